# Optimizing a Trainium2 kernel written in Bass

```python
import jax, jax.numpy as jnp
from jax import lax
import numpy as np

D_MODEL = 1024
BATCH = 4
SEQ = 8192
DEPTH = 1
DEC_BATCH = 128
DEC_SEQ = 8
PAST_LEN = 8192
PAGE_SIZE = 128

D_MIX = D_MODEL
A_WIDTH = D_MIX // 2
A_GROUPS = 4
A_GROUP_DIM = A_WIDTH // A_GROUPS
CHUNK = 128
B_WIDTH = D_MIX - A_WIDTH
B_HEADS = 8
B_HEAD_DIM = B_WIDTH // B_HEADS
ROT_DIM = B_HEAD_DIM // 4
ROPE_THETA = 500000.0
DILATED_CONFIGS = ((128, 1), (512, 4), (2048, 16))
WIN_MAX = 2048
MAX_DIL = 16
BAND = 128
EPS = 1e-6
IN_COLS = 3 * A_WIDTH + 4 * B_WIDTH

kernel_name = "hymba_gmlp_dilated_attn_step"


def _rmsnorm(x, g):
    xf = x.astype(jnp.float32)
    y = xf * lax.rsqrt(jnp.mean(xf * xf, axis=-1, keepdims=True) + EPS)
    return (y * g.astype(jnp.float32)).astype(x.dtype)


def _layernorm(x, g, b):
    xf = x.astype(jnp.float32)
    mu = jnp.mean(xf, axis=-1, keepdims=True)
    xc = xf - mu
    y = xc * lax.rsqrt(jnp.mean(xc * xc, axis=-1, keepdims=True) + EPS)
    return (y * g.astype(jnp.float32) + b.astype(jnp.float32)).astype(x.dtype)


def _rope_partial(x, pos):
    inv = ROPE_THETA ** (-jnp.arange(0, ROT_DIM, 2, dtype=jnp.float32) / ROT_DIM)
    ang = pos.astype(jnp.float32)[:, None] * inv[None, :]
    cos = jnp.cos(ang)[None, :, None, :]
    sin = jnp.sin(ang)[None, :, None, :]
    xf = x.astype(jnp.float32)
    x1 = xf[..., :ROT_DIM // 2]
    x2 = xf[..., ROT_DIM // 2:ROT_DIM]
    rot = jnp.concatenate([x1 * cos - x2 * sin, x1 * sin + x2 * cos], axis=-1).astype(x.dtype)
    return jnp.concatenate([rot, x[..., ROT_DIM:]], axis=-1)


def _project(x, pos, norm_g, w_in, ln_v_g, ln_v_b):
    h = _rmsnorm(x, norm_g)
    z = jnp.einsum('bsd,dc->bsc', h, w_in)
    u, va, ga, q, k, vb, gb = jnp.split(
        z, [A_WIDTH, 2 * A_WIDTH, 3 * A_WIDTH, 3 * A_WIDTH + B_WIDTH,
            3 * A_WIDTH + 2 * B_WIDTH, 3 * A_WIDTH + 3 * B_WIDTH], axis=-1)
    u = jax.nn.gelu(u)
    va = _layernorm(jax.nn.gelu(va), ln_v_g, ln_v_b)
    Bn, S, _ = x.shape
    q = _rope_partial(q.reshape(Bn, S, B_HEADS, B_HEAD_DIM), pos)
    k = _rope_partial(k.reshape(Bn, S, B_HEADS, B_HEAD_DIM), pos)
    vb = vb.reshape(Bn, S, B_HEADS, B_HEAD_DIM)
    return u, va, ga, q, k, vb, gb


def _spatial_gate(u, va, w_spatial, b_spatial):
    n = va.shape[2]
    w = w_spatial[:, :n, :n] * jnp.tril(jnp.ones((n, n), w_spatial.dtype))
    s = jnp.einsum('gij,bcjgf->bcigf', w, va) + b_spatial[:, :n].T[None, None, :, :, None]
    return u * s


def _dilated_branch_prompt(q, k, v, dil, n_strides):
    Bn, Sp, H, E = q.shape
    n = Sp // dil
    nb = n // BAND

    def blocks(x):
        return x.reshape(Bn, n, dil, H, E).transpose(0, 2, 3, 1, 4).reshape(Bn, dil, H, nb, BAND, E)

    def with_prev(x):
        prev = jnp.pad(x, ((0, 0), (0, 0), (0, 0), (1, 0), (0, 0), (0, 0)))[:, :, :, :-1]
        return jnp.concatenate([prev, x], axis=4)

    qb = blocks(q)
    kc = with_prev(blocks(k))
    vc = with_prev(blocks(v))
    s = jnp.einsum('bdhnqe,bdhnke->bdhnqk', qb, kc).astype(jnp.float32) * (E ** -0.5)
    qi = jnp.arange(BAND)[:, None]
    kk = jnp.arange(2 * BAND)[None, :]
    dist = qi + BAND - kk
    valid = (dist >= 0) & (dist <= n_strides)
    has_prev = jnp.arange(nb)[:, None, None] > 0
    mask = valid[None] & (has_prev | (kk >= BAND)[None])
    s = jnp.where(mask, s, -jnp.inf)
    m = jnp.max(s, axis=-1, keepdims=True)
    p = jnp.exp(s - m)
    den = jnp.sum(p, axis=-1)
    o = jnp.einsum('bdhnqk,bdhnke->bdhnqe', p, vc.astype(jnp.float32)) / den[..., None]
    lse = m[..., 0] + jnp.log(den)
    o = o.reshape(Bn, dil, H, n, E).transpose(0, 3, 1, 2, 4).reshape(Bn, Sp, H, E)
    lse = lse.reshape(Bn, dil, H, n).transpose(0, 3, 1, 2).reshape(Bn, Sp, H)
    return o, lse


def _dilated_branch_sample(q, k_all, v_all, dil, n_strides):
    T = q.shape[1]
    E = q.shape[-1]
    w_buf = k_all.shape[1] - T
    idx = w_buf + jnp.arange(T)[:, None] - dil * jnp.arange(n_strides + 1)[None, :]
    valid = idx >= 0
    idx = jnp.maximum(idx, 0)
    kg = k_all[:, idx]
    vg = v_all[:, idx]
    s = jnp.einsum('bthe,btjhe->bthj', q, kg).astype(jnp.float32) * (E ** -0.5)
    s = jnp.where(valid[None, :, None, :], s, -jnp.inf)
    m = jnp.max(s, axis=-1, keepdims=True)
    p = jnp.exp(s - m)
    den = jnp.sum(p, axis=-1)
    o = jnp.einsum('bthj,btjhe->bthe', p, vg.astype(jnp.float32)) / den[..., None]
    lse = m[..., 0] + jnp.log(den)
    return o, lse


def _combine(outs, lses):
    w = jax.nn.softmax(jnp.stack(lses, axis=0), axis=0)
    return jnp.sum(w[..., None] * jnp.stack(outs, axis=0), axis=0)


def _finish(x, a_out, ga, b_out, gb, gn_a, gn_b, w_out, final_g):
    Bn, S, _ = x.shape
    b_out = b_out.reshape(Bn, S, B_WIDTH).astype(x.dtype)
    ha = _rmsnorm(a_out * jax.nn.silu(ga), gn_a)
    hb = _rmsnorm(b_out * jax.nn.silu(gb), gn_b)
    y = x + jnp.einsum('bsc,cd->bsd', jnp.concatenate([ha, hb], axis=-1), w_out)
    return _rmsnorm(y, final_g)


def setup_inputs(seed: int = 0) -> dict:
    key = jax.random.key(seed)
    ks = jax.random.split(key, 16)
    w_buf = min(WIN_MAX, PAST_LEN)
    nrm = jax.random.normal
    f32 = jnp.float32
    return {
        "x_prompt": nrm(ks[0], (BATCH, SEQ, D_MODEL), f32),
        "x_sample": nrm(ks[1], (DEC_BATCH, DEC_SEQ, D_MODEL), f32),
        "cache_k_win": nrm(ks[2], (DEC_BATCH, w_buf, B_HEADS, B_HEAD_DIM), f32),
        "cache_v_win": nrm(ks[3], (DEC_BATCH, w_buf, B_HEADS, B_HEAD_DIM), f32),
        "norm_g": 1.0 + 0.1 * nrm(ks[4], (D_MODEL,), f32),
        "w_in": nrm(ks[5], (D_MODEL, IN_COLS), f32) * D_MODEL ** -0.5,
        "ln_v_g": 1.0 + 0.1 * nrm(ks[6], (A_WIDTH,), f32),
        "ln_v_b": 0.1 * nrm(ks[7], (A_WIDTH,), f32),
        "w_spatial": 0.5 * nrm(ks[8], (A_GROUPS, CHUNK, CHUNK), f32) * CHUNK ** -0.5,
        "b_spatial": 1.0 + 0.1 * nrm(ks[9], (A_GROUPS, CHUNK), f32),
        "gn_a": 1.0 + 0.1 * nrm(ks[10], (A_WIDTH,), f32),
        "gn_b": 1.0 + 0.1 * nrm(ks[11], (B_WIDTH,), f32),
        "w_out": nrm(ks[12], (D_MIX, D_MODEL), f32) * D_MIX ** -0.5,
        "final_g": 1.0 + 0.1 * nrm(ks[13], (D_MODEL,), f32),
    }


def reference(x_prompt, x_sample, cache_k_win, cache_v_win, norm_g, w_in, ln_v_g, ln_v_b,
              w_spatial, b_spatial, gn_a, gn_b, w_out, final_g):
    Bp, S, _ = x_prompt.shape
    pos_p = jnp.arange(S)
    u, va, ga, q, k, vb, gb = _project(x_prompt, pos_p, norm_g, w_in, ln_v_g, ln_v_b)
    nC = S // CHUNK
    a_p = _spatial_gate(u.reshape(Bp, nC, CHUNK, A_GROUPS, A_GROUP_DIM),
                        va.reshape(Bp, nC, CHUNK, A_GROUPS, A_GROUP_DIM),
                        w_spatial, b_spatial).reshape(Bp, S, A_WIDTH)
    unit = BAND * MAX_DIL
    s_pad = -(-S // unit) * unit
    padw = ((0, 0), (0, s_pad - S), (0, 0), (0, 0))
    qp, kp, vp = jnp.pad(q, padw), jnp.pad(k, padw), jnp.pad(vb, padw)
    outs, lses = [], []
    for win, dil in DILATED_CONFIGS:
        o, l = _dilated_branch_prompt(qp, kp, vp, dil, win // dil)
        outs.append(o[:, :S])
        lses.append(l[:, :S])
    b_p = _combine(outs, lses)
    y_prompt = _finish(x_prompt, a_p, ga, b_p, gb, gn_a, gn_b, w_out, final_g)
    w_keep = min(WIN_MAX, S)
    k_win_prompt = k[:, S - w_keep:]
    v_win_prompt = vb[:, S - w_keep:]

    Bd, T, _ = x_sample.shape
    pos_s = PAST_LEN + jnp.arange(T)
    us, vas, gas, qs, kns, vns, gbs = _project(x_sample, pos_s, norm_g, w_in, ln_v_g, ln_v_b)
    a_s = _spatial_gate(us.reshape(Bd, 1, T, A_GROUPS, A_GROUP_DIM),
                        vas.reshape(Bd, 1, T, A_GROUPS, A_GROUP_DIM),
                        w_spatial, b_spatial).reshape(Bd, T, A_WIDTH)
    k_all = jnp.concatenate([cache_k_win.astype(kns.dtype), kns], axis=1)
    v_all = jnp.concatenate([cache_v_win.astype(vns.dtype), vns], axis=1)
    outs_s, lses_s = [], []
    for win, dil in DILATED_CONFIGS:
        o, l = _dilated_branch_sample(qs, k_all, v_all, dil, win // dil)
        outs_s.append(o)
        lses_s.append(l)
    b_s = _combine(outs_s, lses_s)
    y_sample = _finish(x_sample, a_s, gas, b_s, gbs, gn_a, gn_b, w_out, final_g)

    return (y_prompt, y_sample, k_win_prompt, v_win_prompt, kns, vns, vas)
```

```python
import math
import os
from contextlib import ExitStack

import numpy as np
import concourse.bass as bass
import concourse.mybir as mybir
from concourse.bass_utils import run_bass_kernel_spmd

F32 = mybir.dt.float32
BF16 = mybir.dt.bfloat16
AF = mybir.ActivationFunctionType
ALU = mybir.AluOpType

NCORES = 8
D = 1024
NCOL = 3584
EPS = 1e-6
GC = 0.044715
GK = math.sqrt(2.0 / math.pi) * GC
NT_ALL = 49
N_DMA_SEMS = 96


class Sched:
    ENGS = ("pe", "act", "dve", "pool", "sp")

    def __init__(self, same_engine_sync=True):
        self.same_engine_sync = same_engine_sync
        self.ops = {e: [] for e in self.ENGS}
        self.count = {e: 0 for e in self.ENGS}
        self.known = {e: {} for e in self.ENGS}
        self.last_w = {}
        self.reads = {}
        self.dma_total = {}
        self.dma_last = {}
        self.sem_names = set(self.ENGS)
        self.dma_rr = 0
        self.excl = set()

    def _deps(self, reads, writes, eng=None):
        deps = set()
        for k in reads:
            ev = self.last_w.get(k)
            if ev is not None:
                deps.add(ev)
            if k in self.excl:
                for ev in self.reads.get(k, ()):
                    if ev[0] != eng:
                        deps.add(ev)
        for k in writes:
            ev = self.last_w.get(k)
            if ev is not None:
                deps.add(ev)
            for ev in self.reads.get(k, ()):
                deps.add(ev)
        return deps

    def _commit(self, ev, reads, writes):
        for k in reads:
            lst = self.reads.setdefault(k, [])
            lst[:] = [e for e in lst if e[0] != ev[0]] + [ev]
        for k in writes:
            self.last_w[k] = ev
            self.reads[k] = []

    def _waits(self, eng, deps):
        best = {}
        for s, v in deps:
            if s == eng and (eng == "pe" or not self.same_engine_sync):
                continue
            if v > best.get(s, 0):
                best[s] = v
        waits = []
        kn = self.known[eng]
        for s, v in best.items():
            if kn.get(s, 0) >= v:
                continue
            kn[s] = v
            waits.append((s, v))
        return waits

    def op(self, eng, fn, reads=(), writes=()):
        waits = self._waits(eng, self._deps(reads, writes, eng))
        self.count[eng] += 1
        ev = (eng, self.count[eng])
        self.ops[eng].append((waits, fn, eng, 1))
        self._commit(ev, reads, writes)
        return ev

    def dma(self, eng, fn, reads=(), writes=()):
        sem = "dq%d" % (self.dma_rr % N_DMA_SEMS)
        self.dma_rr += 1
        self.sem_names.add(sem)
        deps = self._deps(reads, writes, eng)
        prev = self.dma_last.get(sem)
        if prev is not None:
            deps.add(prev)
        waits = self._waits(eng, deps)
        tot = self.dma_total.get(sem, 0) + 16
        self.dma_total[sem] = tot
        ev = (sem, tot)
        self.dma_last[sem] = ev
        self.ops[eng].append((waits, fn, sem, 16))
        self._commit(ev, reads, writes)
        return ev

    def barrier(self):
        evs = [(e, self.count[e]) for e in self.ENGS if self.count[e] > 0]
        evs += [(s, t) for s, t in self.dma_total.items()]
        for eng in self.ENGS:
            waits = self._waits(eng, set(ev for ev in evs if ev[0] != eng))
            if waits:
                self.ops[eng].append((waits, None, None, 0))
        self.last_w = {}
        self.reads = {}

    def emit(self, block, sems):
        def make(eng):
            ops = self.ops[eng]

            def body(e):
                for waits, fn, semkey, inc in ops:
                    for s, v in waits:
                        e.wait_ge(sems[s], v)
                    if fn is not None:
                        fn(e).then_inc(sems[semkey], inc)
            return body

        block.tensor(make("pe"))
        block.scalar(make("act"))
        block.vector(make("dve"))
        block.gpsimd(make("pool"))
        block.sync(make("sp"))


class Buf:
    __slots__ = ("ap", "key")

    def __init__(self, ap, key):
        self.ap = ap
        self.key = key


class Ring:
    def __init__(self, bufs):
        self.bufs = bufs
        self.i = 0

    def next(self):
        b = self.bufs[self.i % len(self.bufs)]
        self.i += 1
        return b


class Arena:
    def __init__(self, ap2d, name):
        self.ap = ap2d
        self.size = ap2d.shape[1]
        self.off = 0
        self.name = name
        self.gen = 0

    def reset(self):
        self.off = 0
        self.gen += 1

    def take(self, n, tag):
        assert self.off + n <= self.size, (self.name, tag, self.off, n, self.size)
        v = self.ap[:, self.off:self.off + n]
        key = (self.name, self.gen, tag, self.off)
        self.off += n
        return Buf(v, key)


def build_nc(phases="ABCS", same_engine_sync=True, dbg=None):
    nc = bass.Bass("TRN2", target_bir_lowering=False)
    S = Sched(same_engine_sync)
    CUT = (dbg or {}).get("cut", 99)
    RCUT = (dbg or {}).get("rcut", 99)
    CCUT = (dbg or {}).get("ccut", 99)

    def din(name, shape, dt=F32):
        return nc.dram_tensor(name, list(shape), dt, kind="ExternalInput").ap()

    def dout(name, shape):
        return nc.dram_tensor(name, list(shape), F32, kind="ExternalOutput").ap()

    def dscr(name, shape, dt):
        return nc.dram_tensor(name, list(shape), dt).ap()

    xp = din("xp", [6144, D])
    xs = din("xs", [128, D])
    ck = din("ck", [16, 2048, 512] if dbg is None else [1, 8, 512])
    cv = din("cv", [16, 2048, 512] if dbg is None else [1, 8, 512])
    w_in = din("w_in", [D, NCOL])
    w_out = din("w_out", [D, D])
    g_in = din("g_in", [128, 8])
    g_out = din("g_out", [128, 8])
    lng_d = din("lng", [128, 512])
    lnb_d = din("lnb", [128, 512])
    fing_d = din("fing", [128, D])
    wspT_d = din("wspT", [128, 4, 128])
    wsps_d = din("wsps", [128, 4, 128])
    mtril_d = din("mtril", [128, 128])
    mtrils_d = din("mtrils", [128, 128])
    bsp_d = din("bsp", [128, 4])
    bsps_d = din("bsps", [128, 4])
    rope_d = din("rope", [128, NT_ALL, 32])
    vld_d = din("vld", [128, 48])
    bmask_d = din("bmask", [128, 256])
    smask_d = din("smask", [128, 13, 8])
    ident_d = din("ident", [128, 128])

    y_o = dout("y", [4096, D])
    ys_o = dout("ys", [128, D])
    kwin_o = dout("kwin", [2048, 512])
    vwin_o = dout("vwin", [2048, 512])
    kn_o = dout("kn", [128, 512])
    vn_o = dout("vn", [128, 512])
    vas_o = dout("vas", [128, 512])

    v_scr = dscr("v_scr", [6144, 520], BF16)
    ta_scr = dscr("ta_scr", [33 * 128, 512], F32)
    sgb_scr = dscr("sgb_scr", [33 * 128, 512], F32)
    o_scr = dscr("o_scr", [3, 4096, 520], F32)
    os_scr = dscr("os_scr", [128, 520], F32)

    with ExitStack() as es:
        def sb(name, shape, dt):
            return es.enter_context(nc.sbuf_tensor(name, list(shape), dt))

        def ps(name, shape, dt):
            return es.enter_context(nc.psum_tensor(name, list(shape), dt))

        w_in_bf = sb("w_in_bf", [128, 8, NCOL], BF16)
        w_out_bf = sb("w_out_bf", [128, 8, D], BF16)
        KT = sb("KT", [128, 4, 4096], BF16)
        QT = sb("QT", [128, 4, 2048], BF16)
        g_in_sb = sb("g_in_sb", [128, 8], F32)
        g_out_sb = sb("g_out_sb", [128, 8], F32)
        lng = sb("lng_sb", [128, 512], F32)
        lnb = sb("lnb_sb", [128, 512], F32)
        fing = sb("fing_sb", [128, D], F32)
        wsp_bf = sb("wsp_bf", [128, 4, 128], BF16)
        wsps_bf = sb("wsps_bf", [128, 4, 128], BF16)
        bsp = sb("bsp_sb", [128, 4], F32)
        bsps = sb("bsps_sb", [128, 4], F32)
        rope = sb("rope_sb", [128, NT_ALL, 32], F32)
        vld = sb("vld_sb", [128, 48], F32)
        bmask_bf = sb("bmask_bf", [128, 256], BF16)
        smask = sb("smask_sb", [128, 13, 8], F32)
        ident_f = sb("ident_f", [128, 128], F32)
        ident_bf = sb("ident_bf", [128, 128], BF16)
        mhalf = sb("mhalf", [128, 1], F32)
        scal = sb("scal", [128, 96], F32)
        a32_t = sb("arena32", [128, 10240], F32)
        a16_t = sb("arena16", [128, 8704], BF16)
        A32 = Arena(a32_t[:], "a32")
        A16 = Arena(a16_t[:], "a16")

        psA = ps("psA", [128, 1024], F32)
        psB = ps("psB", [128, 1024], F32)
        psO = ps("psO", [128, 1024], F32)
        pb0 = ps("pb0", [128, 1024], BF16)
        pb1 = ps("pb1", [128, 1024], BF16)

        PS = dict(psA0=Buf(psA[:, 0:512], "psA0"), psA1=Buf(psA[:, 512:1024], "psA1"),
                  psB0=Buf(psB[:, 0:512], "psB0"), psB1=Buf(psB[:, 512:1024], "psB1"),
                  psO0=Buf(psO[:, 0:512], "psO0"), psO1=Buf(psO[:, 512:1024], "psO1"),
                  pb0=Buf(pb0[:], "pb0"), pb1=Buf(pb1[:], "pb1"))
        for b_ in PS.values():
            S.excl.add(b_.key)
        fence = {"ap": psO[0:2, 1022:1024], "wide": psO[0:2, 512:768], "key": "psO1"}

        sc_i = [0]

        def new_scalar():
            i = sc_i[0] % 96
            sc_i[0] += 1
            return Buf(scal[:, i:i + 1], ("scal", i))

        R, W = "reads", "writes"

        def OP(eng, fn, reads=(), writes=()):
            return S.op(eng, fn, reads=[b.key if isinstance(b, Buf) else b for b in reads],
                        writes=[b.key if isinstance(b, Buf) else b for b in writes])

        def DMA(fn, reads=(), writes=(), eng="sp"):
            return S.dma(eng, fn, reads=[b.key if isinstance(b, Buf) else b for b in reads],
                         writes=[b.key if isinstance(b, Buf) else b for b in writes])

        PEFENCE = (dbg or {}).get("pefence", 1)

        def pe_fence(bufs, wide=True):
            if not PEFENCE:
                return
            if wide:
                fap = fence["wide"]
                OP("pe", lambda e: e.matmul(fap, lhsT=ident_bf[:, 0:2], rhs=w_out_bf[:, 0, 0:256],
                                            start=True, stop=True),
                   ["ident_bf"], list(bufs) + [fence["key"]])
            else:
                fap = fence["ap"]
                OP("pe", lambda e: e.matmul(fap, lhsT=ident_bf[:, 0:2], rhs=ident_bf[:, 0:2], start=True, stop=True),
                   ["ident_bf"], list(bufs) + [fence["key"]])

        def rstd_chain(ssq, mult, add_eps, post):
            p2 = post * post
            var = new_scalar()
            OP("dve", lambda e: e.tensor_scalar(out=var.ap, in0=ssq.ap, scalar1=mult / p2, scalar2=add_eps / p2,
                                                op0=ALU.mult, op1=ALU.add), [ssq], [var])
            r = new_scalar()
            OP("pool", lambda e: e.tensor_tensor(out=r.ap, in0=var.ap, in1=mhalf[:], op=ALU.pow),
               [var, "mhalf"], [r])
            return r

        A32.reset()
        A16.reset()
        for dst, src, key in ((g_in_sb, g_in, "g_in"), (g_out_sb, g_out, "g_out"), (lng, lng_d, "lng"),
                              (lnb, lnb_d, "lnb"), (fing, fing_d, "fing"), (bsp, bsp_d, "bsp"),
                              (bsps, bsps_d, "bsps"), (rope, rope_d, "rope"), (vld, vld_d, "vld"),
                              (smask, smask_d, "smask"), (ident_f, ident_d, "ident_f")):
            DMA(lambda e, dst=dst, src=src: e.dma_start(out=dst[:], in_=src), writes=[key])
        OP("pool", lambda e: e.memset(mhalf[:], -0.5), writes=["mhalf"])
        OP("dve", lambda e: e.tensor_copy(out=ident_bf[:], in_=ident_f[:]), ["ident_f"], ["ident_bf"])
        st_a = A32.take(512, "st_a")
        st_b = A32.take(512, "st_b")
        st_m = A32.take(128, "st_m")
        st_ms = A32.take(128, "st_ms")
        st_bm = A32.take(256, "st_bm")
        DMA(lambda e: e.dma_start(out=st_a.ap.rearrange("p (g i) -> p g i", i=128), in_=wspT_d), writes=[st_a])
        DMA(lambda e: e.dma_start(out=st_b.ap.rearrange("p (g i) -> p g i", i=128), in_=wsps_d), writes=[st_b])
        DMA(lambda e: e.dma_start(out=st_m.ap, in_=mtril_d), writes=[st_m])
        DMA(lambda e: e.dma_start(out=st_ms.ap, in_=mtrils_d), writes=[st_ms])
        DMA(lambda e: e.dma_start(out=st_bm.ap, in_=bmask_d), writes=[st_bm])
        OP("dve", lambda e: e.tensor_tensor(out=wsp_bf[:], in0=st_a.ap.rearrange("p (g i) -> p g i", i=128),
                                            in1=st_m.ap.unsqueeze(1).broadcast_to([128, 4, 128]), op=ALU.mult),
           [st_a, st_m], ["wsp_bf"])
        OP("dve", lambda e: e.tensor_tensor(out=wsps_bf[:], in0=st_b.ap.rearrange("p (g i) -> p g i", i=128),
                                            in1=st_ms.ap.unsqueeze(1).broadcast_to([128, 4, 128]), op=ALU.mult),
           [st_b, st_ms], ["wsps_bf"])
        OP("dve", lambda e: e.tensor_copy(out=bmask_bf[:], in_=st_bm.ap), [st_bm], ["bmask_bf"])
        stg = [A32.take(NCOL, "wst0"), A32.take(NCOL, "wst1")]
        for kc in range(8):
            st = stg[kc % 2]
            DMA(lambda e, st=st, kc=kc: e.dma_start(out=st.ap, in_=w_in[kc * 128:(kc + 1) * 128, :]), writes=[st])
            OP("act", lambda e, st=st, kc=kc: e.activation(out=w_in_bf[:, kc, 0:1536], in_=st.ap[:, 0:1536],
                                                           func=AF.Copy, scale=g_in_sb[:, kc:kc + 1]),
               [st, "g_in"], [("w_in_bf", kc, 0)])
            OP("dve", lambda e, st=st, kc=kc: e.tensor_scalar(out=w_in_bf[:, kc, 1536:3072], in0=st.ap[:, 1536:3072],
                                                              scalar1=g_in_sb[:, kc:kc + 1], scalar2=None,
                                                              op0=ALU.mult), [st, "g_in"], [("w_in_bf", kc, 1)])
            OP("pool", lambda e, st=st, kc=kc: e.tensor_scalar(out=w_in_bf[:, kc, 3072:3584], in0=st.ap[:, 3072:3584],
                                                               scalar1=g_in_sb[:, kc:kc + 1], scalar2=0.0,
                                                               op0=ALU.mult, op1=ALU.add),
               [st, "g_in"], [("w_in_bf", kc, 2)])
        for kc in range(8):
            st = stg[kc % 2]
            DMA(lambda e, st=st, kc=kc: e.dma_start(out=st.ap[:, 0:D], in_=w_out[kc * 128:(kc + 1) * 128, :]), writes=[st])
            OP("act", lambda e, st=st, kc=kc: e.activation(out=w_out_bf[:, kc, 0:512], in_=st.ap[:, 0:512],
                                                           func=AF.Copy, scale=g_out_sb[:, kc:kc + 1]),
               [st, "g_out"], [("w_out_bf", kc, 0)])
            OP("dve", lambda e, st=st, kc=kc: e.tensor_scalar(out=w_out_bf[:, kc, 512:1024], in0=st.ap[:, 512:1024],
                                                              scalar1=g_out_sb[:, kc:kc + 1], scalar2=None,
                                                              op0=ALU.mult), [st, "g_out"], [("w_out_bf", kc, 1)])
        S.barrier()

        def phase_A(tiles):
            A32.reset()
            A16.reset()
            xt_r = Ring([A32.take(1024, "xt%d" % i) for i in range(2)])
            t512 = Ring([A32.take(512, "t%d" % i) for i in range(8)])
            v2_r = Ring([A32.take(512, "v2_%d" % i) for i in range(2)])
            ug_r = Ring([A32.take(512, "ug_%d" % i) for i in range(2)])
            kf_r = Ring([A32.take(512, "kf%d" % i) for i in range(2)])
            ab_r = Ring([A32.take(128, "ab%d" % i) for i in range(4)])
            st_r = Ring([A32.take(8, "bst%d" % i) for i in range(2)])
            xn_r = Ring([A16.take(1024, "xn%d" % i) for i in range(2)])
            xT_r = Ring([A16.take(1024, "xT%d" % i) for i in range(2)])
            qkb_r = Ring([A16.take(1024, "qkb%d" % i) for i in range(2)])
            vab_r = Ring([A16.take(512, "vab%d" % i) for i in range(2)])
            vex_r = Ring([A16.take(520, "vex%d" % i) for i in range(2)])
            banks = Ring([PS["psA0"], PS["psA1"], PS["psB0"], PS["psB1"]])
            ps_sp = PS["psO0"]
            p_xT = PS["pb0"]
            p_qk = PS["pb1"]
            fence.update(ap=psO[0:2, 1022:1024], wide=psO[0:2, 512:768], key="psO1")

            def load(t):
                xt = xt_r.next()
                DMA(lambda e: e.dma_start(out=xt.ap, in_=t["src"]), writes=[xt])
                t["xt"] = xt

            def norm(t):
                xt = t["xt"]
                if CUT < 1:
                    return
                xn = xn_r.next()
                ssq = new_scalar()
                OP("act", lambda e: e.activation(out=xn.ap, in_=xt.ap, func=AF.Square, accum_out=ssq.ap),
                   [xt], [xn, ssq])
                rstd = rstd_chain(ssq, 1.0 / D, EPS, 1.0)
                OP("act", lambda e: e.activation(out=xn.ap, in_=xt.ap, func=AF.Copy, scale=rstd.ap),
                   [xt, rstd], [xn])
                t["xn"] = xn

            def normB(t):
                xn = t["xn"]
                for k in range(8):
                    OP("pe", lambda e, k=k: e.transpose(out=p_xT.ap[:, k * 128:(k + 1) * 128],
                                                         in_=xn.ap[:, k * 128:(k + 1) * 128], identity=ident_bf[:]),
                       [xn, "ident_bf"], [p_xT])
                pe_fence([p_xT])
                xT = xT_r.next()
                OP("dve", lambda e: e.tensor_copy(out=xT.ap, in_=p_xT.ap), [p_xT], [xT])
                t["xT"] = xT

            def mm_group(t, cg):
                bank = banks.next()
                xT = t["xT"]
                for kc in range(8):
                    OP("pe", lambda e, kc=kc: e.matmul(bank.ap, lhsT=xT.ap[:, kc * 128:(kc + 1) * 128],
                                                        rhs=w_in_bf[:, kc, cg * 512:(cg + 1) * 512],
                                                        start=(kc == 0), stop=(kc == 7)),
                       [xT, ("w_in_bf", kc, 0), ("w_in_bf", kc, 1), ("w_in_bf", kc, 2)], [bank])
                pe_fence([bank])
                return bank

            def rope_evac(t, bank, dst2, dst_buf):
                dst3 = dst2.rearrange("p (h e) -> p h e", e=64)
                z3 = bank.ap.rearrange("p (h e) -> p h e", e=64)
                gt = t["gt"]
                cc = rope[:, gt, 0:16].unsqueeze(1).broadcast_to([128, 8, 16])
                ss = rope[:, gt, 16:32].unsqueeze(1).broadcast_to([128, 8, 16])
                if (dbg or {}).get("ropecopy", "act") == "dve":
                    OP("dve", lambda e: e.tensor_copy(out=dst2, in_=bank.ap), [bank], [dst_buf])
                else:
                    if (dbg or {}).get("ropecopy", "act") == "act_delay":
                        dl = new_scalar()
                        OP("act", lambda e: e.activation(out=dl.ap, in_=mhalf[:], func=AF.Copy), [bank, "mhalf"], [dl])
                    OP("act", lambda e: e.activation(out=dst2, in_=bank.ap, func=AF.Copy), [bank], [dst_buf])
                if RCUT < 2:
                    return
                a = ab_r.next()
                b = ab_r.next()
                a3 = a.ap.rearrange("p (h e) -> p h e", e=16)
                b3 = b.ap.rearrange("p (h e) -> p h e", e=16)
                OP("dve", lambda e: e.tensor_tensor(out=a3, in0=z3[:, :, 0:16], in1=cc, op=ALU.mult),
                   [bank, "rope"], [a])
                OP("dve", lambda e: e.tensor_tensor(out=b3, in0=z3[:, :, 0:16], in1=ss, op=ALU.mult),
                   [bank, "rope"], [b])
                if RCUT < 3:
                    return
                OP("pool", lambda e: e.tensor_tensor(out=dst3[:, :, 0:8], in0=a3[:, :, 0:8], in1=b3[:, :, 8:16],
                                                     op=ALU.subtract), [a, b], [dst_buf])
                OP("pool", lambda e: e.tensor_tensor(out=dst3[:, :, 8:16], in0=a3[:, :, 8:16], in1=b3[:, :, 0:8],
                                                     op=ALU.add), [a, b], [dst_buf])

            def gelu2(bank):
                sq = t512.next()
                OP("act", lambda e: e.activation(out=sq.ap, in_=bank.ap, func=AF.Square), [bank], [sq])
                w = t512.next()
                OP("dve", lambda e: e.scalar_tensor_tensor(out=w.ap, in0=sq.ap, scalar=1.0 / GC, in1=bank.ap,
                                                           op0=ALU.add, op1=ALU.mult), [sq, bank], [w])
                th = t512.next()
                OP("act", lambda e: e.activation(out=th.ap, in_=w.ap, func=AF.Tanh, scale=GK), [w], [th])
                o = t512.next()
                OP("dve", lambda e: e.scalar_tensor_tensor(out=o.ap, in0=th.ap, scalar=1.0, in1=bank.ap,
                                                           op0=ALU.add, op1=ALU.mult), [th, bank], [o])
                return o

            def silu2(bank):
                th = t512.next()
                OP("act", lambda e: e.activation(out=th.ap, in_=bank.ap, func=AF.Tanh, scale=0.5), [bank], [th])
                o = t512.next()
                OP("dve", lambda e: e.scalar_tensor_tensor(out=o.ap, in0=th.ap, scalar=1.0, in1=bank.ap,
                                                           op0=ALU.add, op1=ALU.mult), [th, bank], [o])
                return o

            def m1(t):
                kind = t["kind"]
                gt = t["gt"]
                full = kind != "halo"
                qkb = qkb_r.next()
                t["qkb"] = qkb
                if full:
                    bq = mm_group(t, 3)
                    rope_evac(t, bq, qkb.ap[:, 0:512], qkb)
                bk = mm_group(t, 4)
                if t.get("k_out") is not None:
                    kf = kf_r.next()
                    rope_evac(t, bk, kf.ap, kf)
                    OP("pool", lambda e: e.tensor_copy(out=qkb.ap[:, 512:1024], in_=kf.ap), [kf], [qkb])
                    DMA(lambda e: e.dma_start(out=t["k_out"], in_=kf.ap), reads=[kf])
                else:
                    rope_evac(t, bk, qkb.ap[:, 512:1024], qkb)
                bv = mm_group(t, 5)
                if t.get("v_out") is not None:
                    vf = t512.next()
                    OP("act", lambda e: e.activation(out=vf.ap, in_=bv.ap, func=AF.Copy), [bv], [vf])
                    DMA(lambda e: e.dma_start(out=t["v_out"], in_=vf.ap), reads=[vf])
                if kind != "sample":
                    vex = vex_r.next()
                    vex3 = vex.ap.rearrange("p (h e) -> p h e", e=65)
                    OP("dve", lambda e: e.tensor_copy(out=vex3[:, :, 0:64],
                                                      in_=bv.ap.rearrange("p (h e) -> p h e", e=64)), [bv], [vex])
                    OP("pool", lambda e: e.tensor_copy(out=vex3[:, :, 64:65],
                                                       in_=vld[:, gt:gt + 1].unsqueeze(1).broadcast_to([128, 8, 1])),
                       ["vld"], [vex])
                    DMA(lambda e: e.dma_start(out=v_scr[gt * 128:(gt + 1) * 128, :], in_=vex.ap), reads=[vex])
                if not full:
                    return
                bva = mm_group(t, 1)
                sq = t512.next()
                OP("act", lambda e: e.activation(out=sq.ap, in_=bva.ap, func=AF.Square), [bva], [sq])
                w = t512.next()
                OP("dve", lambda e: e.scalar_tensor_tensor(out=w.ap, in0=sq.ap, scalar=1.0 / GC, in1=bva.ap,
                                                           op0=ALU.add, op1=ALU.mult), [sq, bva], [w])
                th = t512.next()
                OP("act", lambda e: e.activation(out=th.ap, in_=w.ap, func=AF.Tanh, scale=GK), [w], [th])
                v2 = v2_r.next()
                OP("dve", lambda e: e.scalar_tensor_tensor(out=v2.ap, in0=th.ap, scalar=1.0, in1=bva.ap,
                                                           op0=ALU.add, op1=ALU.mult), [th, bva], [v2])
                st = st_r.next()
                OP("dve", lambda e: e.bn_stats(out=st.ap[:, 0:6], in_=v2.ap), [v2], [st])
                OP("dve", lambda e: e.bn_aggr(out=st.ap[:, 6:8], in_=st.ap[:, 0:6]), [st], [st])
                rh = rstd_chain(Buf(st.ap[:, 7:8], st.key), 0.25, EPS, 0.5)
                t.update(v2=v2, st=st, rh=rh)
                bu = mm_group(t, 0)
                u2 = gelu2(bu)
                bga = mm_group(t, 2)
                ga2 = silu2(bga)
                ug = ug_r.next()
                OP("pool", lambda e: e.tensor_tensor(out=ug.ap, in0=u2.ap, in1=ga2.ap, op=ALU.mult), [u2, ga2], [ug])
                t["ug"] = ug
                bgb = mm_group(t, 6)
                gb2 = silu2(bgb)
                ot = t["ot"]
                DMA(lambda e: e.dma_start(out=sgb_scr[ot * 128:(ot + 1) * 128, :], in_=gb2.ap), reads=[gb2])

            def m2a(t):
                if t["kind"] != "halo":
                    v2, st, rh = t["v2"], t["st"], t["rh"]
                    vn0 = t512.next()
                    OP("dve", lambda e: e.tensor_scalar(out=vn0.ap, in0=v2.ap, scalar1=st.ap[:, 6:7], scalar2=rh.ap,
                                                        op0=ALU.subtract, op1=ALU.mult), [v2, st, rh], [vn0])
                    vn1 = t512.next()
                    OP("pool", lambda e: e.tensor_tensor(out=vn1.ap, in0=vn0.ap, in1=lng[:], op=ALU.mult),
                       [vn0, "lng"], [vn1])
                    van = t512.next()
                    OP("pool", lambda e: e.tensor_tensor(out=van.ap, in0=vn1.ap, in1=lnb[:], op=ALU.add),
                       [vn1, "lnb"], [van])
                    vab = vab_r.next()
                    OP("act", lambda e: e.activation(out=vab.ap, in_=van.ap, func=AF.Copy), [van], [vab])
                    if t.get("va_out") is not None:
                        DMA(lambda e: e.dma_start(out=t["va_out"], in_=van.ap), reads=[van])
                    t["vab"] = vab

            def m2b(t):
                kind = t["kind"]
                full = kind != "halo"
                qkb = t["qkb"]
                vab = t.get("vab")
                lo = 0 if full else 4
                for j in range(lo, 8):
                    OP("pe", lambda e, j=j: e.transpose(out=p_qk.ap[:, j * 128:(j + 1) * 128],
                                                         in_=qkb.ap[:, j * 128:(j + 1) * 128],
                                                         identity=ident_bf[:]), [qkb, "ident_bf"], [p_qk])
                pe_fence([p_qk])
                if full:
                    qc = t["qt_col"]
                    OP("dve", lambda e: e.tensor_copy(out=QT[:, :, qc:qc + 128],
                                                      in_=p_qk.ap[:, 0:512].rearrange("p (j t) -> p j t", t=128)),
                       [p_qk], [("QT", qc)])
                kc0 = t["kt_col"]
                OP("act", lambda e: e.activation(out=KT[:, :, kc0:kc0 + 128],
                                                 in_=p_qk.ap[:, 512:1024].rearrange("p (j t) -> p j t", t=128),
                                                 func=AF.Copy), [p_qk], [("KT", kc0)])
                if not full:
                    return
                ug = t["ug"]
                wsp = wsps_bf if kind == "sample" else wsp_bf
                wkey = "wsps_bf" if kind == "sample" else "wsp_bf"
                for g in range(4):
                    OP("pe", lambda e, g=g: e.matmul(ps_sp.ap[:, g * 128:(g + 1) * 128], lhsT=wsp[:, g, :],
                                                      rhs=vab.ap[:, g * 128:(g + 1) * 128], start=True, stop=True),
                       [vab, wkey], [ps_sp])
                pe_fence([ps_sp])
                ta = t512.next()
                bs = bsps if kind == "sample" else bsp
                bskey = "bsps" if kind == "sample" else "bsp"
                for g in range(4):
                    OP("dve", lambda e, g=g: e.scalar_tensor_tensor(
                        out=ta.ap[:, g * 128:(g + 1) * 128], in0=ps_sp.ap[:, g * 128:(g + 1) * 128],
                        scalar=bs[:, g:g + 1], in1=ug.ap[:, g * 128:(g + 1) * 128], op0=ALU.add, op1=ALU.mult),
                       [ps_sp, ug, bskey], [ta])
                ot = t["ot"]
                DMA(lambda e: e.dma_start(out=ta_scr[ot * 128:(ot + 1) * 128, :], in_=ta.ap), reads=[ta])

            n = len(tiles)
            for i in range(n + 3):
                if i < n:
                    load(tiles[i])
                if 0 <= i - 1 < n:
                    norm(tiles[i - 1])
                if 0 <= i - 3 < n:
                    m2a(tiles[i - 3])
                if 0 <= i - 2 < n:
                    m1(tiles[i - 2])
                if 0 <= i - 1 < n:
                    normB(tiles[i - 1])
                if 0 <= i - 3 < n:
                    m2b(tiles[i - 3])
            S.barrier()

        SLOT = [0, 2, 1, 3]

        def phase_B(ui):
            A32.reset()
            A16.reset()
            pt_r = Ring([A16.take(1024, "pt%d" % i) for i in range(3)])
            vb_r = Ring([A16.take(520, "vb%d" % i) for i in range(8)])
            osb_r = Ring([A32.take(520, "osb%d" % i) for i in range(3)])
            s_bufs = [(psA[:], [PS["psA0"], PS["psA1"]]), (psB[:], [PS["psB0"], PS["psB1"]])]
            pO_ap = psO[:]
            pO_k = [PS["psO0"], PS["psO1"]]
            fence.update(ap=pb1[0:2, 1020:1024].bitcast(F32), wide=pb1[0:2, 0:512].bitcast(F32), key="pb1")

            work = []
            for c, d in ((0, 1), (1, 4), (2, 16)):
                nqb = 2048 // d // 128
                for r in range(d):
                    for qb in range(nqb):
                        work.append((c, d, r, qb))

            loaded = {}

            def kt_cols(d, r, kb):
                local = r + d * 128 * kb
                u = ui
                if local < 0:
                    local += 2048
                    u = ui - 1
                base = (u % 2) * 2048 + local
                return base, u * 2048 + local

            def ensure(c, d, r, kb):
                key = (c, r, kb)
                if key in loaded:
                    return loaded[key]
                vb = vb_r.next()
                for k2 in [k for k, v in loaded.items() if v is vb]:
                    del loaded[k2]
                _, g0 = kt_cols(d, r, kb)
                DMA(lambda e: e.dma_start(out=vb.ap, in_=v_scr[g0:g0 + d * 127 + 1:d, :]), writes=[vb])
                loaded[key] = vb
                return vb

            items = []
            for w, (c, d, r, qb) in enumerate(work):
                for g in range(2):
                    items.append((w, g))
            st_ = {}

            def stage_S(idx):
                w, g = items[idx]
                c, d, r, qb = work[w]
                if g == 0:
                    ensure(c, d, r, qb - 1)
                    ensure(c, d, r, qb)
                    for w2 in (w + 1, w + 2):
                        if w2 < len(work):
                            c2, d2, r2, qb2 = work[w2]
                            ensure(c2, d2, r2, qb2 - 1)
                            ensure(c2, d2, r2, qb2)
                sb_ap, sb_keys = s_bufs[idx % 2]
                s3 = sb_ap.rearrange("p (s n) -> p s n", n=256)
                q0 = r + d * 128 * qb
                kp, _ = kt_cols(d, r, qb - 1)
                kc, _ = kt_cols(d, r, qb)
                for hh in range(4):
                    h = 4 * g + hh
                    pr = h // 2
                    r0 = (h % 2) * 64
                    sl = SLOT[hh]
                    qap = QT[r0:r0 + 64, pr, q0:q0 + d * 127 + 1:d]
                    OP("pe", lambda e, sl=sl, r0=r0, pr=pr, qap=qap: e.matmul(
                        s3[:, sl, 0:128], lhsT=KT[r0:r0 + 64, pr, kp:kp + d * 127 + 1:d], rhs=qap,
                        start=True, stop=True), ["QTall", "KTall"], [sb_keys[sl // 2]])
                    OP("pe", lambda e, sl=sl, r0=r0, pr=pr, qap=qap: e.matmul(
                        s3[:, sl, 128:256], lhsT=KT[r0:r0 + 64, pr, kc:kc + d * 127 + 1:d], rhs=qap,
                        start=True, stop=True), ["QTall", "KTall"], [sb_keys[sl // 2]])
                pe_fence(sb_keys, wide=False)
                pt = pt_r.next()
                OP("act", lambda e: e.activation(out=pt.ap, in_=sb_ap, func=AF.Exp, scale=0.125), sb_keys, [pt])
                pt3 = pt.ap.rearrange("p (s n) -> p s n", n=256)
                OP("dve", lambda e: e.tensor_tensor(out=pt3, in0=pt3,
                                                    in1=bmask_bf[:].unsqueeze(1).broadcast_to([128, 4, 256]),
                                                    op=ALU.mult), [pt, "bmask_bf"], [pt])
                st_[idx] = pt

            def stage_P(idx):
                w, g = items[idx]
                c, d, r, qb = work[w]
                pt = st_.pop(idx)
                pt3 = pt.ap.rearrange("p (s n) -> p s n", n=256)
                vp = loaded[(c, r, qb - 1)]
                vc = loaded[(c, r, qb)]
                vp3 = vp.ap.rearrange("p (h e) -> p h e", e=65)
                vc3 = vc.ap.rearrange("p (h e) -> p h e", e=65)
                for hh in range(4):
                    h = 4 * g + hh
                    sl = SLOT[hh]
                    oc = g * 512 + hh * 65
                    OP("pe", lambda e, sl=sl, h=h, oc=oc, hh=hh: e.matmul(
                        pO_ap[:, oc:oc + 65], lhsT=pt3[:, sl, 0:128], rhs=vp3[:, h, :],
                        start=(hh == 0), stop=False, skip_group_check=True), [pt, vp], [pO_k[g]])
                    OP("pe", lambda e, sl=sl, h=h, oc=oc: e.matmul(
                        pO_ap[:, oc:oc + 65], lhsT=pt3[:, sl, 128:256], rhs=vc3[:, h, :],
                        start=False, stop=True, skip_group_check=True), [pt, vc], [pO_k[g]])
                pe_fence([pO_k[g]])
                if g == 1:
                    osb = osb_r.next()
                    OP("act", lambda e: e.activation(out=osb.ap[:, 0:260], in_=pO_ap[:, 0:260], func=AF.Copy),
                       [pO_k[0]], [(osb.key, 0)])
                    OP("dve", lambda e: e.tensor_copy(out=osb.ap[:, 260:520], in_=pO_ap[:, 512:772]),
                       [pO_k[1]], [(osb.key, 1)])
                    q0 = (ui - 1) * 2048 + r + d * 128 * qb
                    DMA(lambda e: e.dma_start(out=o_scr[c, q0:q0 + d * 127 + 1:d, :], in_=osb.ap),
                        reads=[(osb.key, 0), (osb.key, 1)])

            n = len(items)
            for idx in range(n + 1):
                if idx < n:
                    stage_S(idx)
                if idx - 1 >= 0:
                    stage_P(idx - 1)
            S.barrier()

        def phase_C(tiles):
            A32.reset()
            A16.reset()
            xt_r = Ring([A32.take(1024, "cx%d" % i) for i in range(2)])
            ta_r = Ring([A32.take(512, "cta%d" % i) for i in range(2)])
            sg_r = Ring([A32.take(512, "csg%d" % i) for i in range(2)])
            o_r = Ring([A32.take(1560, "co%d" % i) for i in range(2)])
            tb_r = Ring([A32.take(512, "ctb%d" % i) for i in range(1)])
            y_r = Ring([A32.take(1024, "cy%d" % i) for i in range(2)])
            rd_r = Ring([A32.take(8, "crd%d" % i) for i in range(2)])
            hc_r = Ring([A16.take(1024, "chc%d" % i) for i in range(2)])
            hT_r = Ring([A16.take(1024, "chT%d" % i) for i in range(2)])
            jk = A16.take(1024, "cjunk")
            jk2 = A16.take(1024, "cjunk2")
            p_hT = PS["pb0"]
            ybanks = Ring([(psA[:], [PS["psA0"], PS["psA1"]]), (psB[:], [PS["psB0"], PS["psB1"]])])
            fence.update(ap=pb1[0:2, 1020:1024].bitcast(F32), wide=pb1[0:2, 0:512].bitcast(F32), key="pb1")

            def load(t):
                ta = ta_r.next()
                sg = sg_r.next()
                ob = o_r.next()
                ot = t["ot"]
                DMA(lambda e: e.dma_start(out=ta.ap, in_=ta_scr[ot * 128:(ot + 1) * 128, :]), writes=[ta])
                DMA(lambda e: e.dma_start(out=sg.ap, in_=sgb_scr[ot * 128:(ot + 1) * 128, :]), writes=[sg])
                if t["kind"] == "sample":
                    DMA(lambda e: e.dma_start(out=ob.ap[:, 0:520], in_=os_scr), writes=[ob])
                else:
                    for c in range(3):
                        DMA(lambda e, c=c: e.dma_start(out=ob.ap[:, c * 520:(c + 1) * 520],
                                                       in_=o_scr[c, ot * 128:(ot + 1) * 128, :]), writes=[ob])
                t.update(ta=ta, sg=sg, ob=ob)

            def load_x(t):
                xt = xt_r.next()
                DMA(lambda e: e.dma_start(out=xt.ap, in_=t["src"]), writes=[xt])
                t["xt"] = xt

            def stage1(t):
                ta, sg, ob = t["ta"], t["sg"], t["ob"]
                if t["kind"] != "sample":
                    OP("dve", lambda e: e.tensor_tensor(out=ob.ap[:, 0:520], in0=ob.ap[:, 0:520],
                                                        in1=ob.ap[:, 520:1040], op=ALU.add), [ob], [ob])
                    OP("dve", lambda e: e.tensor_tensor(out=ob.ap[:, 0:520], in0=ob.ap[:, 0:520],
                                                        in1=ob.ap[:, 1040:1560], op=ALU.add), [ob], [ob])
                o3 = ob.ap[:, 0:520].rearrange("p (h e) -> p h e", e=65)
                rd = rd_r.next()
                OP("dve", lambda e: e.reciprocal(out=rd.ap.unsqueeze(2), in_=o3[:, :, 64:65]), [ob], [rd])
                tb = tb_r.next()
                tb3 = tb.ap.rearrange("p (h e) -> p h e", e=64)
                OP("dve", lambda e: e.tensor_tensor(out=tb3, in0=o3[:, :, 0:64],
                                                    in1=rd.ap.unsqueeze(2).broadcast_to([128, 8, 64]), op=ALU.mult),
                   [ob, rd], [tb])
                OP("pool", lambda e: e.tensor_tensor(out=tb.ap, in0=tb.ap, in1=sg.ap, op=ALU.mult), [tb, sg], [tb])
                ssa = new_scalar()
                ssb = new_scalar()
                OP("act", lambda e: e.activation(out=jk.ap[:, 0:512], in_=ta.ap, func=AF.Square, accum_out=ssa.ap),
                   [ta], [jk, ssa])
                OP("act", lambda e: e.activation(out=jk.ap[:, 512:1024], in_=tb.ap, func=AF.Square, accum_out=ssb.ap),
                   [tb], [jk, ssb])
                ra = rstd_chain(ssa, 1.0 / (16.0 * 512.0), EPS, 0.25)
                rb = rstd_chain(ssb, 1.0 / (4.0 * 512.0), EPS, 0.5)
                hc = hc_r.next()
                OP("act", lambda e: e.activation(out=hc.ap[:, 0:512], in_=ta.ap, func=AF.Copy, scale=ra.ap),
                   [ta, ra], [hc])
                OP("act", lambda e: e.activation(out=hc.ap[:, 512:1024], in_=tb.ap, func=AF.Copy, scale=rb.ap),
                   [tb, rb], [hc])
                t["hc"] = hc

            def stage2(t):
                hc, xt = t["hc"], t["xt"]
                for k in range(8):
                    OP("pe", lambda e, k=k: e.transpose(out=p_hT.ap[:, k * 128:(k + 1) * 128],
                                                         in_=hc.ap[:, k * 128:(k + 1) * 128], identity=ident_bf[:]),
                       [hc, "ident_bf"], [p_hT])
                pe_fence([p_hT])
                hT = hT_r.next()
                OP("dve", lambda e: e.tensor_copy(out=hT.ap, in_=p_hT.ap), [p_hT], [hT])
                yb_ap, yb_k = ybanks.next()
                for cg in range(2):
                    for kc in range(8):
                        OP("pe", lambda e, cg=cg, kc=kc: e.matmul(
                            yb_ap[:, cg * 512:(cg + 1) * 512], lhsT=hT.ap[:, kc * 128:(kc + 1) * 128],
                            rhs=w_out_bf[:, kc, cg * 512:(cg + 1) * 512], start=(kc == 0), stop=(kc == 7)),
                           [hT, ("w_out_bf", kc, 0), ("w_out_bf", kc, 1)], [yb_k[cg]])
                pe_fence(yb_k)
                y = y_r.next()
                for cg in range(2):
                    OP("dve", lambda e, cg=cg: e.tensor_tensor(out=y.ap[:, cg * 512:(cg + 1) * 512],
                                                               in0=yb_ap[:, cg * 512:(cg + 1) * 512],
                                                               in1=xt.ap[:, cg * 512:(cg + 1) * 512], op=ALU.add),
                       [yb_k[cg], xt], [(y.key, cg)])
                ssy = new_scalar()
                yk = [(y.key, 0), (y.key, 1)]
                OP("act", lambda e: e.activation(out=jk2.ap, in_=y.ap, func=AF.Square, accum_out=ssy.ap),
                   yk, [jk2, ssy])
                t["ry"] = rstd_chain(ssy, 1.0 / D, EPS, 1.0)
                t["y"] = y

            def stage3(t):
                y, ry = t["y"], t["ry"]
                yk = [(y.key, 0), (y.key, 1)]
                OP("act", lambda e: e.activation(out=y.ap, in_=y.ap, func=AF.Copy, scale=ry.ap), yk + [ry], yk)
                OP("pool", lambda e: e.tensor_tensor(out=y.ap, in0=y.ap, in1=fing[:], op=ALU.mult),
                   yk + ["fing"], yk)
                DMA(lambda e: e.dma_start(out=t["dst"], in_=y.ap), reads=yk)

            n = len(tiles)
            for st in range(n + 3):
                if st < n:
                    load(tiles[st])
                if 0 <= st - 1 < n:
                    load_x(tiles[st - 1])
                    stage1(tiles[st - 1])
                if 0 <= st - 2 < n:
                    stage2(tiles[st - 2])
                if 0 <= st - 3 < n:
                    stage3(tiles[st - 3])
            S.barrier()

        def phase_SA():
            A32.reset()
            A16.reset()
            kc_r = Ring([A32.take(512, "skc%d" % i) for i in range(8)])
            vc_r = Ring([A32.take(512, "svc%d" % i) for i in range(8)])
            os_r = Ring([A32.take(520, "sos%d" % i) for i in range(2)])
            pf_r = Ring([A32.take(64, "spf%d" % i) for i in range(3)])
            kts_r = Ring([A16.take(512, "skt%d" % i) for i in range(3)])
            vbs = [A16.take(520, "svb%d" % i) for i in range(4)]
            vb_r = Ring(vbs)
            pm_r = Ring([A16.take(64, "spm%d" % i) for i in range(3)])
            qbd = A16.take(4 * 16 * 16, "qbd")
            qbd4 = qbd.ap.rearrange("p (j b c) -> p j b c", j=4, b=16)
            ktp = Ring([PS["psA0"], PS["psA1"]])
            sps = Ring([Buf(psB[:, 0:64], "psB0"), Buf(psB[:, 512:576], "psB1")])
            pO_ap = psO[:]
            pO_k = [PS["psO0"], PS["psO1"]]
            fence.update(ap=pb1[0:2, 1020:1024].bitcast(F32), wide=pb1[0:2, 0:512].bitcast(F32), key="pb1")

            OP("pool", lambda e: e.memset(qbd.ap, 0.0), writes=[qbd])
            q4 = QT[:, :, 0:128].rearrange("p j (b t) -> p j b t", t=8)
            OP("dve", lambda e: e.tensor_copy(out=qbd4[0:64, :, :, 0:8], in_=q4[0:64]), ["QTall", qbd], [qbd])
            OP("dve", lambda e: e.tensor_copy(out=qbd4[64:128, :, :, 8:16], in_=q4[64:128]), ["QTall", qbd], [qbd])
            for vb in vbs:
                OP("pool", lambda e, vb=vb: e.memset(vb.ap, 1.0), writes=[vb])

            items = []
            for b in range(16):
                for t0 in range(8):
                    items.append((b, len(items) % 13, 96, ck[b, t0:1536:16, :], cv[b, t0:1536:16, :]))
                for c in range(4):
                    items.append((b, 8 + c, 128, ck[b, 1536 + 128 * c:1664 + 128 * c, :],
                                  cv[b, 1536 + 128 * c:1664 + 128 * c, :]))
                items.append((b, 12, 8, kn_o[b * 8:(b + 1) * 8, :], vn_o[b * 8:(b + 1) * 8, :]))
            n_it = len(items)
            stt = [dict() for _ in range(n_it)]

            def st_load(j):
                b, kt, nk, ksrc, vsrc = items[j]
                kc = kc_r.next()
                vc = vc_r.next()
                DMA(lambda e: e.dma_start(out=kc.ap[0:nk, :], in_=ksrc), writes=[kc])
                DMA(lambda e: e.dma_start(out=vc.ap[0:nk, :], in_=vsrc), writes=[vc])
                stt[j].update(kc=kc, vc=vc)

            def st_T(j):
                b, kt, nk, ksrc, vsrc = items[j]
                kc, vc = stt[j]["kc"], stt[j]["vc"]
                kp = ktp.next()
                for q in range(4):
                    OP("pe", lambda e, q=q: e.transpose(
                        out=kp.ap[:, q * 128:q * 128 + nk], in_=kc.ap[0:nk, q * 128:(q + 1) * 128],
                        identity=ident_f[0:nk, 0:nk]), [kc, "ident_f"], [kp])
                pe_fence([kp])
                kts = kts_r.next()
                kts3 = kts.ap.rearrange("p (j n) -> p j n", n=128)
                OP("act", lambda e: e.activation(
                    out=kts3[:, :, 0:nk], in_=kp.ap.rearrange("p (j n) -> p j n", n=128)[:, :, 0:nk],
                    func=AF.Copy), [kp], [kts])
                vb = vb_r.next()
                OP("dve", lambda e: e.tensor_copy(
                    out=vb.ap[0:nk, :].rearrange("p (h e) -> p h e", e=65)[:, :, 0:64],
                    in_=vc.ap[0:nk, :].rearrange("p (h e) -> p h e", e=64)), [vc], [vb])
                stt[j].update(kts=kts, kts3=kts3, vb=vb)

            def st_S(j):
                b, kt, nk, ksrc, vsrc = items[j]
                kts, kts3 = stt[j]["kts"], stt[j]["kts3"]
                sp_ = sps.next()
                for q in range(4):
                    OP("pe", lambda e, q=q: e.matmul(
                        sp_.ap[0:nk, q * 16:(q + 1) * 16], lhsT=kts3[:, q, 0:nk], rhs=qbd4[:, q, b, :],
                        start=True, stop=True), [kts, qbd], [sp_])
                pe_fence([sp_])
                pf = pf_r.next()
                OP("act", lambda e: e.activation(out=pf.ap[0:nk, :], in_=sp_.ap[0:nk, :], func=AF.Exp, scale=0.125),
                   [sp_], [pf])
                pm = pm_r.next()
                OP("dve", lambda e: e.tensor_tensor(
                    out=pm.ap[0:nk, :].rearrange("p (h t) -> p h t", t=8),
                    in0=pf.ap[0:nk, :].rearrange("p (h t) -> p h t", t=8),
                    in1=smask[0:nk, kt, :].unsqueeze(1).broadcast_to([nk, 8, 8]), op=ALU.mult),
                   [pf, "smask"], [pm])
                stt[j].update(pm=pm)

            def st_PV(j):
                b, kt, nk, ksrc, vsrc = items[j]
                pm, vb = stt[j]["pm"], stt[j]["vb"]
                vb3 = vb.ap.rearrange("p (h e) -> p h e", e=65)
                for h in range(8):
                    oc = (h // 4) * 512 + (h % 4) * 65
                    OP("pe", lambda e, h=h, oc=oc: e.matmul(
                        pO_ap[0:8, oc:oc + 65], lhsT=pm.ap[0:nk, h * 8:(h + 1) * 8], rhs=vb3[0:nk, h, :],
                        start=(kt == 0 and h % 4 == 0), stop=(kt == 12), skip_group_check=True),
                       [pm, vb], [pO_k[h // 4]])
                if kt == 12:
                    pe_fence(pO_k)
                    osb = os_r.next()
                    OP("act", lambda e: e.activation(out=osb.ap[0:8, 0:260], in_=pO_ap[0:8, 0:260], func=AF.Copy),
                       [pO_k[0]], [(osb.key, 0)])
                    OP("dve", lambda e: e.tensor_copy(out=osb.ap[0:8, 260:520], in_=pO_ap[0:8, 512:772]),
                       [pO_k[1]], [(osb.key, 1)])
                    DMA(lambda e: e.dma_start(out=os_scr[b * 8:(b + 1) * 8, :], in_=osb.ap[0:8, :]),
                        reads=[(osb.key, 0), (osb.key, 1)])
                stt[j] = None

            LA = 6
            for j in range(min(LA, n_it)):
                st_load(j)
            for st in range(n_it + 2):
                if st + LA < n_it:
                    st_load(st + LA)
                if st < n_it:
                    st_T(st)
                if 0 <= st - 1 < n_it:
                    st_S(st - 1)
                if 0 <= st - 2 < n_it:
                    st_PV(st - 2)
            S.barrier()

        def own_tile(ot):
            gt = 16 + ot
            ui = 1 + ot // 16
            t = dict(kind="own", gt=gt, ot=ot, src=xp[gt * 128:(gt + 1) * 128, :],
                     kt_col=(ui % 2) * 2048 + (ot % 16) * 128, qt_col=(ot % 16) * 128,
                     dst=y_o[ot * 128:(ot + 1) * 128, :])
            if ot >= 16:
                t["k_out"] = kwin_o[(ot - 16) * 128:(ot - 15) * 128, :]
                t["v_out"] = vwin_o[(ot - 16) * 128:(ot - 15) * 128, :]
            return t

        halo = [dict(kind="halo", gt=i, src=xp[i * 128:(i + 1) * 128, :], kt_col=i * 128) for i in range(16)]
        samp = dict(kind="sample", gt=48, ot=32, src=xs, kt_col=0, qt_col=0, k_out=kn_o, v_out=vn_o,
                    va_out=vas_o, dst=ys_o)

        if dbg is not None:
            if dbg.get("halo", 0):
                phase_A(halo[:dbg["halo"]])
            if dbg.get("own"):
                phase_A([own_tile(ot) for ot in dbg["own"]])
            if dbg.get("samp"):
                phase_A([samp])
            if dbg.get("C"):
                phase_C([own_tile(ot) for ot in dbg["C"]])
            S.barrier()
            sems = {n: es.enter_context(nc.semaphore(n)) for n in sorted(S.sem_names)}
            with nc.Block() as block:
                S.emit(block, sems)
            return nc
        if "A" in phases:
            phase_A(halo)
        for u in range(int(os.environ.get("K_NUNITS", "2"))):
            tl = [own_tile(ot) for ot in range(16 * u, 16 * u + 16)]
            if "A" in phases:
                phase_A(tl)
            if "B" in phases:
                phase_B(u + 1)
            if "C" in phases:
                phase_C(tl)
        if "S" in phases:
            phase_A([samp])
            phase_SA()
            phase_C([samp])
        S.barrier()

        sems = {n: es.enter_context(nc.semaphore(n)) for n in sorted(S.sem_names)}
        with nc.Block() as block:
            S.emit(block, sems)
    return nc


def _rope_table(core):
    half = core % 2
    pos = np.zeros((NT_ALL, 128), np.float32)
    p = np.arange(128, dtype=np.float32)
    for ti in range(16):
        pos[ti] = (2048 + ti * 128 + p) if half == 1 else 0.0
    for ot in range(32):
        pos[16 + ot] = half * 4096 + ot * 128 + p
    pos[48] = 8192 + (np.arange(128) % 8)
    inv = (np.float32(500000.0) ** (-np.arange(0, 16, 2, dtype=np.float32) / np.float32(16))).astype(np.float32)
    ang = (pos[:, :, None] * inv[None, None, :]).astype(np.float32)
    cos = np.cos(ang).astype(np.float32)
    sin = np.sin(ang).astype(np.float32)
    tab = np.concatenate([cos, cos, sin, sin], axis=-1)
    return np.ascontiguousarray(tab.transpose(1, 0, 2))


def _sample_mask():
    m = np.zeros((128, 13, 8), np.float32)
    t = np.arange(8)
    for t0 in range(8):
        m[:96, t0, t0] = 1.0
    for c in range(4):
        i = np.arange(128)
        dist = 512 + t[None, :] - 128 * c - i[:, None]
        cnt = ((dist >= 0) & (dist <= 128)).astype(np.float32)
        cnt += ((dist >= 0) & (dist % 4 == 0) & (dist <= 512)).astype(np.float32)
        cnt += ((dist >= 0) & (dist % 16 == 0) & (dist <= 2048)).astype(np.float32)
        m[:, 8 + c, :] = cnt
    tp = np.arange(8)
    dist = t[None, :] - tp[:, None]
    cnt = (dist >= 0).astype(np.float32) + ((dist >= 0) & (dist % 4 == 0)) + ((dist >= 0) & (dist % 16 == 0))
    m[:8, 12, :] = cnt
    return m


_NC_CACHE = {}


def kernel(x_prompt, x_sample, cache_k_win, cache_v_win, norm_g, w_in, ln_v_g, ln_v_b,
           w_spatial, b_spatial, gn_a, gn_b, w_out, final_g, _phases="ABCS", _same_engine_sync=True):
    f32 = np.float32
    x_prompt = np.asarray(x_prompt, f32)
    x_sample = np.asarray(x_sample, f32)
    cache_k_win = np.asarray(cache_k_win, f32)
    cache_v_win = np.asarray(cache_v_win, f32)
    w_in = np.ascontiguousarray(np.asarray(w_in, f32))
    w_out = np.ascontiguousarray(np.asarray(w_out, f32))
    w_spatial = np.asarray(w_spatial, f32)
    b_spatial = np.asarray(b_spatial, f32)

    key = (_phases, _same_engine_sync)
    if key not in _NC_CACHE:
        _NC_CACHE[key] = build_nc(_phases, _same_engine_sync)
    nc = _NC_CACHE[key]

    g_in = np.ascontiguousarray(np.asarray(norm_g, f32).reshape(8, 128).T)
    g_out = np.ascontiguousarray(np.concatenate([np.asarray(gn_a, f32), np.asarray(gn_b, f32)]).reshape(8, 128).T)
    lng = np.ascontiguousarray(np.broadcast_to(np.asarray(ln_v_g, f32)[None, :], (128, 512)))
    lnb = np.ascontiguousarray(np.broadcast_to(np.asarray(ln_v_b, f32)[None, :], (128, 512)))
    fing = np.ascontiguousarray(np.broadcast_to(np.asarray(final_g, f32)[None, :], (128, 1024)))
    wspT = np.ascontiguousarray(w_spatial.transpose(2, 0, 1))
    wsps = np.zeros((128, 4, 128), f32)
    w8 = w_spatial[:, :8, :8].transpose(2, 0, 1)
    for b in range(16):
        wsps[b * 8:(b + 1) * 8, :, b * 8:(b + 1) * 8] = w8
    jj = np.arange(128)
    mtril = (jj[:, None] <= jj[None, :]).astype(f32)
    mtrils = ((jj[:, None] // 8 == jj[None, :] // 8) & (jj[:, None] % 8 <= jj[None, :] % 8)).astype(f32)
    bsp = np.ascontiguousarray(b_spatial.T)
    bsps = np.ascontiguousarray(np.tile(b_spatial[:, :8].T, (16, 1)))
    bmask = np.concatenate([(jj[:, None] >= jj[None, :]), (jj[:, None] <= jj[None, :])], axis=1).astype(f32)
    smask = _sample_mask()
    ident = np.eye(128, dtype=f32)

    in_maps = []
    for c in range(NCORES):
        b, half = c // 2, c % 2
        xp = np.empty((6144, D), f32)
        if half == 1:
            xp[:2048] = x_prompt[b, 2048:4096]
            vld = np.ones((128, 48), f32)
        else:
            xp[:2048] = 0.0
            vld = np.ones((128, 48), f32)
            vld[:, :16] = 0.0
        xp[2048:] = x_prompt[b, half * 4096:(half + 1) * 4096]
        in_maps.append(dict(
            xp=xp, xs=np.ascontiguousarray(x_sample[16 * c:16 * c + 16].reshape(128, D)),
            ck=np.ascontiguousarray(cache_k_win[16 * c:16 * c + 16].reshape(16, 2048, 512)),
            cv=np.ascontiguousarray(cache_v_win[16 * c:16 * c + 16].reshape(16, 2048, 512)),
            w_in=w_in, w_out=w_out, g_in=g_in, g_out=g_out, lng=lng, lnb=lnb, fing=fing,
            wspT=wspT, wsps=wsps, mtril=mtril, mtrils=mtrils, bsp=bsp, bsps=bsps,
            rope=_rope_table(c), vld=vld, bmask=bmask, smask=smask, ident=ident))

    res = run_bass_kernel_spmd(nc, in_maps, core_ids=list(range(NCORES)))
    R = res.results

    y_prompt = np.empty((4, 8192, D), f32)
    k_win = np.empty((4, 2048, 8, 64), f32)
    v_win = np.empty((4, 2048, 8, 64), f32)
    for c in range(NCORES):
        b, half = c // 2, c % 2
        y_prompt[b, half * 4096:(half + 1) * 4096] = R[c]["y"]
        if half == 1:
            k_win[b] = R[c]["kwin"].reshape(2048, 8, 64)
            v_win[b] = R[c]["vwin"].reshape(2048, 8, 64)
    y_sample = np.concatenate([R[c]["ys"].reshape(16, 8, D) for c in range(NCORES)], axis=0)
    k_new = np.concatenate([R[c]["kn"].reshape(16, 8, 8, 64) for c in range(NCORES)], axis=0)
    v_new = np.concatenate([R[c]["vn"].reshape(16, 8, 8, 64) for c in range(NCORES)], axis=0)
    vas = np.concatenate([R[c]["vas"].reshape(16, 8, 512) for c in range(NCORES)], axis=0)
    return (y_prompt, y_sample, k_win, v_win, k_new, v_new, vas)
```

```python
import math
import os
from contextlib import ExitStack

import numpy as np
import concourse.bass as bass
import concourse.mybir as mybir
from concourse.bass_utils import run_bass_kernel_spmd

F32 = mybir.dt.float32
BF16 = mybir.dt.bfloat16
AF = mybir.ActivationFunctionType
ALU = mybir.AluOpType

NCORES = 8
D = 1024
NCOL = 3584
EPS = 1e-6
GC = 0.044715
GK = math.sqrt(2.0 / math.pi) * GC
NT_ALL = 49
N_DMA_SEMS = 96


class Sched:
    ENGS = ("pe", "act", "dve", "pool", "sp")

    def __init__(self, same_engine_sync=True):
        self.same_engine_sync = same_engine_sync
        self.ops = {e: [] for e in self.ENGS}
        self.count = {e: 0 for e in self.ENGS}
        self.known = {e: {} for e in self.ENGS}
        self.last_w = {}
        self.reads = {}
        self.dma_total = {}
        self.dma_last = {}
        self.sem_names = set(self.ENGS)
        self.dma_rr = 0
        self.excl = set()

    def _deps(self, reads, writes, eng=None):
        deps = set()
        for k in reads:
            ev = self.last_w.get(k)
            if ev is not None:
                deps.add(ev)
            if k in self.excl:
                for ev in self.reads.get(k, ()):
                    if ev[0] != eng:
                        deps.add(ev)
        for k in writes:
            ev = self.last_w.get(k)
            if ev is not None:
                deps.add(ev)
            for ev in self.reads.get(k, ()):
                deps.add(ev)
        return deps

    def _commit(self, ev, reads, writes):
        for k in reads:
            lst = self.reads.setdefault(k, [])
            lst[:] = [e for e in lst if e[0] != ev[0]] + [ev]
        for k in writes:
            self.last_w[k] = ev
            self.reads[k] = []

    def _waits(self, eng, deps):
        best = {}
        for s, v in deps:
            if s == eng and (eng == "pe" or not self.same_engine_sync):
                continue
            if v > best.get(s, 0):
                best[s] = v
        waits = []
        kn = self.known[eng]
        for s, v in best.items():
            if kn.get(s, 0) >= v:
                continue
            kn[s] = v
            waits.append((s, v))
        return waits

    def op(self, eng, fn, reads=(), writes=()):
        waits = self._waits(eng, self._deps(reads, writes, eng))
        self.count[eng] += 1
        ev = (eng, self.count[eng])
        self.ops[eng].append((waits, fn, eng, 1))
        self._commit(ev, reads, writes)
        return ev

    def dma(self, eng, fn, reads=(), writes=()):
        sem = "dq%d" % (self.dma_rr % N_DMA_SEMS)
        self.dma_rr += 1
        self.sem_names.add(sem)
        deps = self._deps(reads, writes, eng)
        prev = self.dma_last.get(sem)
        if prev is not None:
            deps.add(prev)
        waits = self._waits(eng, deps)
        tot = self.dma_total.get(sem, 0) + 16
        self.dma_total[sem] = tot
        ev = (sem, tot)
        self.dma_last[sem] = ev
        self.ops[eng].append((waits, fn, sem, 16))
        self._commit(ev, reads, writes)
        return ev

    def barrier(self):
        evs = [(e, self.count[e]) for e in self.ENGS if self.count[e] > 0]
        evs += [(s, t) for s, t in self.dma_total.items()]
        for eng in self.ENGS:
            waits = self._waits(eng, set(ev for ev in evs if ev[0] != eng))
            if waits:
                self.ops[eng].append((waits, None, None, 0))
        self.last_w = {}
        self.reads = {}

    def emit(self, block, sems):
        def make(eng):
            ops = self.ops[eng]

            def body(e):
                for waits, fn, semkey, inc in ops:
                    for s, v in waits:
                        e.wait_ge(sems[s], v)
                    if fn is not None:
                        fn(e).then_inc(sems[semkey], inc)
            return body

        block.tensor(make("pe"))
        block.scalar(make("act"))
        block.vector(make("dve"))
        block.gpsimd(make("pool"))
        block.sync(make("sp"))


class Buf:
    __slots__ = ("ap", "key")

    def __init__(self, ap, key):
        self.ap = ap
        self.key = key


class Ring:
    def __init__(self, bufs):
        self.bufs = bufs
        self.i = 0

    def next(self):
        b = self.bufs[self.i % len(self.bufs)]
        self.i += 1
        return b


class Arena:
    def __init__(self, ap2d, name):
        self.ap = ap2d
        self.size = ap2d.shape[1]
        self.off = 0
        self.name = name
        self.gen = 0

    def reset(self):
        self.off = 0
        self.gen += 1

    def take(self, n, tag):
        assert self.off + n <= self.size, (self.name, tag, self.off, n, self.size)
        v = self.ap[:, self.off:self.off + n]
        key = (self.name, self.gen, tag, self.off)
        self.off += n
        return Buf(v, key)


def build_nc(phases="ABCS", same_engine_sync=True, dbg=None):
    nc = bass.Bass("TRN2", target_bir_lowering=False)
    S = Sched(same_engine_sync)
    CUT = (dbg or {}).get("cut", 99)
    RCUT = (dbg or {}).get("rcut", 99)
    CCUT = (dbg or {}).get("ccut", 99)

    def din(name, shape, dt=F32):
        return nc.dram_tensor(name, list(shape), dt, kind="ExternalInput").ap()

    def dout(name, shape):
        return nc.dram_tensor(name, list(shape), F32, kind="ExternalOutput").ap()

    def dscr(name, shape, dt):
        return nc.dram_tensor(name, list(shape), dt).ap()

    xp = din("xp", [6144, D])
    xs = din("xs", [128, D])
    ck = din("ck", [16, 2048, 512] if dbg is None else [1, 8, 512])
    cv = din("cv", [16, 2048, 512] if dbg is None else [1, 8, 512])
    w_in = din("w_in", [D, NCOL])
    w_out = din("w_out", [D, D])
    g_in = din("g_in", [128, 8])
    g_out = din("g_out", [128, 8])
    lng_d = din("lng", [128, 512])
    lnb_d = din("lnb", [128, 512])
    fing_d = din("fing", [128, D])
    wspT_d = din("wspT", [128, 4, 128])
    wsps_d = din("wsps", [128, 4, 128])
    mtril_d = din("mtril", [128, 128])
    mtrils_d = din("mtrils", [128, 128])
    bsp_d = din("bsp", [128, 4])
    bsps_d = din("bsps", [128, 4])
    rope_d = din("rope", [128, NT_ALL, 32])
    vld_d = din("vld", [128, 48])
    bmask_d = din("bmask", [128, 256])
    smask_d = din("smask", [128, 13, 8])
    ident_d = din("ident", [128, 128])

    y_o = dout("y", [4096, D])
    ys_o = dout("ys", [128, D])
    kwin_o = dout("kwin", [2048, 512])
    vwin_o = dout("vwin", [2048, 512])
    kn_o = dout("kn", [128, 512])
    vn_o = dout("vn", [128, 512])
    vas_o = dout("vas", [128, 512])

    v_scr = dscr("v_scr", [6144, 520], BF16)
    ta_scr = dscr("ta_scr", [33 * 128, 512], F32)
    sgb_scr = dscr("sgb_scr", [33 * 128, 512], F32)
    o_scr = dscr("o_scr", [3, 4096, 520], F32)
    os_scr = dscr("os_scr", [128, 520], F32)

    with ExitStack() as es:
        def sb(name, shape, dt):
            return es.enter_context(nc.sbuf_tensor(name, list(shape), dt))

        def ps(name, shape, dt):
            return es.enter_context(nc.psum_tensor(name, list(shape), dt))

        w_in_bf = sb("w_in_bf", [128, 8, NCOL], BF16)
        w_out_bf = sb("w_out_bf", [128, 8, D], BF16)
        KT = sb("KT", [128, 4, 4096], BF16)
        QT = sb("QT", [128, 4, 2048], BF16)
        g_in_sb = sb("g_in_sb", [128, 8], F32)
        g_out_sb = sb("g_out_sb", [128, 8], F32)
        lng = sb("lng_sb", [128, 512], F32)
        lnb = sb("lnb_sb", [128, 512], F32)
        fing = sb("fing_sb", [128, D], F32)
        wsp_bf = sb("wsp_bf", [128, 4, 128], BF16)
        wsps_bf = sb("wsps_bf", [128, 4, 128], BF16)
        bsp = sb("bsp_sb", [128, 4], F32)
        bsps = sb("bsps_sb", [128, 4], F32)
        rope = sb("rope_sb", [128, NT_ALL, 32], F32)
        vld = sb("vld_sb", [128, 48], F32)
        bmask_bf = sb("bmask_bf", [128, 256], BF16)
        smask = sb("smask_sb", [128, 13, 8], F32)
        ident_f = sb("ident_f", [128, 128], F32)
        ident_bf = sb("ident_bf", [128, 128], BF16)
        mhalf = sb("mhalf", [128, 1], F32)
        scal = sb("scal", [128, 96], F32)
        a32_t = sb("arena32", [128, 10240], F32)
        a16_t = sb("arena16", [128, 8704], BF16)
        A32 = Arena(a32_t[:], "a32")
        A16 = Arena(a16_t[:], "a16")

        psA = ps("psA", [128, 1024], F32)
        psB = ps("psB", [128, 1024], F32)
        psO = ps("psO", [128, 1024], F32)
        pb0 = ps("pb0", [128, 1024], BF16)
        pb1 = ps("pb1", [128, 1024], BF16)

        PS = dict(psA0=Buf(psA[:, 0:512], "psA0"), psA1=Buf(psA[:, 512:1024], "psA1"),
                  psB0=Buf(psB[:, 0:512], "psB0"), psB1=Buf(psB[:, 512:1024], "psB1"),
                  psO0=Buf(psO[:, 0:512], "psO0"), psO1=Buf(psO[:, 512:1024], "psO1"),
                  pb0=Buf(pb0[:], "pb0"), pb1=Buf(pb1[:], "pb1"))
        for b_ in PS.values():
            S.excl.add(b_.key)
        fence = {"ap": psO[0:2, 1022:1024], "wide": psO[0:2, 512:768], "key": "psO1"}

        sc_i = [0]

        def new_scalar():
            i = sc_i[0] % 96
            sc_i[0] += 1
            return Buf(scal[:, i:i + 1], ("scal", i))

        R, W = "reads", "writes"

        def OP(eng, fn, reads=(), writes=()):
            return S.op(eng, fn, reads=[b.key if isinstance(b, Buf) else b for b in reads],
                        writes=[b.key if isinstance(b, Buf) else b for b in writes])

        def DMA(fn, reads=(), writes=(), eng="sp"):
            return S.dma(eng, fn, reads=[b.key if isinstance(b, Buf) else b for b in reads],
                         writes=[b.key if isinstance(b, Buf) else b for b in writes])

        PEFENCE = (dbg or {}).get("pefence", 1)

        def pe_fence(bufs, wide=True):
            if not PEFENCE:
                return
            if wide:
                fap = fence["wide"]
                OP("pe", lambda e: e.matmul(fap, lhsT=ident_bf[:, 0:2], rhs=w_out_bf[:, 0, 0:256],
                                            start=True, stop=True),
                   ["ident_bf"], list(bufs) + [fence["key"]])
            else:
                fap = fence["ap"]
                OP("pe", lambda e: e.matmul(fap, lhsT=ident_bf[:, 0:2], rhs=ident_bf[:, 0:2], start=True, stop=True),
                   ["ident_bf"], list(bufs) + [fence["key"]])

        def rstd_chain(ssq, mult, add_eps, post):
            p2 = post * post
            var = new_scalar()
            OP("dve", lambda e: e.tensor_scalar(out=var.ap, in0=ssq.ap, scalar1=mult / p2, scalar2=add_eps / p2,
                                                op0=ALU.mult, op1=ALU.add), [ssq], [var])
            r = new_scalar()
            OP("pool", lambda e: e.tensor_tensor(out=r.ap, in0=var.ap, in1=mhalf[:], op=ALU.pow),
               [var, "mhalf"], [r])
            return r

        A32.reset()
        A16.reset()
        for dst, src, key in ((g_in_sb, g_in, "g_in"), (g_out_sb, g_out, "g_out"), (lng, lng_d, "lng"),
                              (lnb, lnb_d, "lnb"), (fing, fing_d, "fing"), (bsp, bsp_d, "bsp"),
                              (bsps, bsps_d, "bsps"), (rope, rope_d, "rope"), (vld, vld_d, "vld"),
                              (smask, smask_d, "smask"), (ident_f, ident_d, "ident_f")):
            DMA(lambda e, dst=dst, src=src: e.dma_start(out=dst[:], in_=src), writes=[key])
        OP("pool", lambda e: e.memset(mhalf[:], -0.5), writes=["mhalf"])
        OP("dve", lambda e: e.tensor_copy(out=ident_bf[:], in_=ident_f[:]), ["ident_f"], ["ident_bf"])
        st_a = A32.take(512, "st_a")
        st_b = A32.take(512, "st_b")
        st_m = A32.take(128, "st_m")
        st_ms = A32.take(128, "st_ms")
        st_bm = A32.take(256, "st_bm")
        DMA(lambda e: e.dma_start(out=st_a.ap.rearrange("p (g i) -> p g i", i=128), in_=wspT_d), writes=[st_a])
        DMA(lambda e: e.dma_start(out=st_b.ap.rearrange("p (g i) -> p g i", i=128), in_=wsps_d), writes=[st_b])
        DMA(lambda e: e.dma_start(out=st_m.ap, in_=mtril_d), writes=[st_m])
        DMA(lambda e: e.dma_start(out=st_ms.ap, in_=mtrils_d), writes=[st_ms])
        DMA(lambda e: e.dma_start(out=st_bm.ap, in_=bmask_d), writes=[st_bm])
        OP("dve", lambda e: e.tensor_tensor(out=wsp_bf[:], in0=st_a.ap.rearrange("p (g i) -> p g i", i=128),
                                            in1=st_m.ap.unsqueeze(1).broadcast_to([128, 4, 128]), op=ALU.mult),
           [st_a, st_m], ["wsp_bf"])
        OP("dve", lambda e: e.tensor_tensor(out=wsps_bf[:], in0=st_b.ap.rearrange("p (g i) -> p g i", i=128),
                                            in1=st_ms.ap.unsqueeze(1).broadcast_to([128, 4, 128]), op=ALU.mult),
           [st_b, st_ms], ["wsps_bf"])
        OP("dve", lambda e: e.tensor_copy(out=bmask_bf[:], in_=st_bm.ap), [st_bm], ["bmask_bf"])
        stg = [A32.take(NCOL, "wst0"), A32.take(NCOL, "wst1")]
        for kc in range(8):
            st = stg[kc % 2]
            DMA(lambda e, st=st, kc=kc: e.dma_start(out=st.ap, in_=w_in[kc * 128:(kc + 1) * 128, :]), writes=[st])
            OP("act", lambda e, st=st, kc=kc: e.activation(out=w_in_bf[:, kc, 0:1536], in_=st.ap[:, 0:1536],
                                                           func=AF.Copy, scale=g_in_sb[:, kc:kc + 1]),
               [st, "g_in"], [("w_in_bf", kc, 0)])
            OP("dve", lambda e, st=st, kc=kc: e.tensor_scalar(out=w_in_bf[:, kc, 1536:3072], in0=st.ap[:, 1536:3072],
                                                              scalar1=g_in_sb[:, kc:kc + 1], scalar2=None,
                                                              op0=ALU.mult), [st, "g_in"], [("w_in_bf", kc, 1)])
            OP("pool", lambda e, st=st, kc=kc: e.tensor_scalar(out=w_in_bf[:, kc, 3072:3584], in0=st.ap[:, 3072:3584],
                                                               scalar1=g_in_sb[:, kc:kc + 1], scalar2=0.0,
                                                               op0=ALU.mult, op1=ALU.add),
               [st, "g_in"], [("w_in_bf", kc, 2)])
        for kc in range(8):
            st = stg[kc % 2]
            DMA(lambda e, st=st, kc=kc: e.dma_start(out=st.ap[:, 0:D], in_=w_out[kc * 128:(kc + 1) * 128, :]), writes=[st])
            OP("act", lambda e, st=st, kc=kc: e.activation(out=w_out_bf[:, kc, 0:512], in_=st.ap[:, 0:512],
                                                           func=AF.Copy, scale=g_out_sb[:, kc:kc + 1]),
               [st, "g_out"], [("w_out_bf", kc, 0)])
            OP("dve", lambda e, st=st, kc=kc: e.tensor_scalar(out=w_out_bf[:, kc, 512:1024], in0=st.ap[:, 512:1024],
                                                              scalar1=g_out_sb[:, kc:kc + 1], scalar2=None,
                                                              op0=ALU.mult), [st, "g_out"], [("w_out_bf", kc, 1)])
        S.barrier()

        def phase_A(tiles):
            A32.reset()
            A16.reset()
            xt_r = Ring([A32.take(1024, "xt%d" % i) for i in range(2)])
            t512 = Ring([A32.take(512, "t%d" % i) for i in range(8)])
            v2_r = Ring([A32.take(512, "v2_%d" % i) for i in range(2)])
            ug_r = Ring([A32.take(512, "ug_%d" % i) for i in range(2)])
            kf_r = Ring([A32.take(512, "kf%d" % i) for i in range(2)])
            ab_r = Ring([A32.take(128, "ab%d" % i) for i in range(4)])
            st_r = Ring([A32.take(8, "bst%d" % i) for i in range(2)])
            xn_r = Ring([A16.take(1024, "xn%d" % i) for i in range(2)])
            xT_r = Ring([A16.take(1024, "xT%d" % i) for i in range(2)])
            qkb_r = Ring([A16.take(1024, "qkb%d" % i) for i in range(2)])
            vab_r = Ring([A16.take(512, "vab%d" % i) for i in range(2)])
            vex_r = Ring([A16.take(520, "vex%d" % i) for i in range(2)])
            banks = Ring([PS["psA0"], PS["psA1"], PS["psB0"], PS["psB1"]])
            ps_sp = PS["psO0"]
            p_xT = PS["pb0"]
            p_qk = PS["pb1"]
            fence.update(ap=psO[0:2, 1022:1024], wide=psO[0:2, 512:768], key="psO1")

            def load(t):
                xt = xt_r.next()
                DMA(lambda e: e.dma_start(out=xt.ap, in_=t["src"]), writes=[xt])
                t["xt"] = xt

            def norm(t):
                xt = t["xt"]
                if CUT < 1:
                    return
                xn = xn_r.next()
                ssq = new_scalar()
                OP("act", lambda e: e.activation(out=xn.ap, in_=xt.ap, func=AF.Square, accum_out=ssq.ap),
                   [xt], [xn, ssq])
                rstd = rstd_chain(ssq, 1.0 / D, EPS, 1.0)
                OP("act", lambda e: e.activation(out=xn.ap, in_=xt.ap, func=AF.Copy, scale=rstd.ap),
                   [xt, rstd], [xn])
                t["xn"] = xn

            def normB(t):
                xn = t["xn"]
                for k in range(8):
                    OP("pe", lambda e, k=k: e.transpose(out=p_xT.ap[:, k * 128:(k + 1) * 128],
                                                         in_=xn.ap[:, k * 128:(k + 1) * 128], identity=ident_bf[:]),
                       [xn, "ident_bf"], [p_xT])
                pe_fence([p_xT])
                xT = xT_r.next()
                OP("dve", lambda e: e.tensor_copy(out=xT.ap, in_=p_xT.ap), [p_xT], [xT])
                t["xT"] = xT

            def mm_group(t, cg):
                bank = banks.next()
                xT = t["xT"]
                for kc in range(8):
                    OP("pe", lambda e, kc=kc: e.matmul(bank.ap, lhsT=xT.ap[:, kc * 128:(kc + 1) * 128],
                                                        rhs=w_in_bf[:, kc, cg * 512:(cg + 1) * 512],
                                                        start=(kc == 0), stop=(kc == 7)),
                       [xT, ("w_in_bf", kc, 0), ("w_in_bf", kc, 1), ("w_in_bf", kc, 2)], [bank])
                pe_fence([bank])
                return bank

            def rope_evac(t, bank, dst2, dst_buf):
                dst3 = dst2.rearrange("p (h e) -> p h e", e=64)
                z3 = bank.ap.rearrange("p (h e) -> p h e", e=64)
                gt = t["gt"]
                cc = rope[:, gt, 0:16].unsqueeze(1).broadcast_to([128, 8, 16])
                ss = rope[:, gt, 16:32].unsqueeze(1).broadcast_to([128, 8, 16])
                if (dbg or {}).get("ropecopy", "act") == "dve":
                    OP("dve", lambda e: e.tensor_copy(out=dst2, in_=bank.ap), [bank], [dst_buf])
                else:
                    if (dbg or {}).get("ropecopy", "act") == "act_delay":
                        dl = new_scalar()
                        OP("act", lambda e: e.activation(out=dl.ap, in_=mhalf[:], func=AF.Copy), [bank, "mhalf"], [dl])
                    OP("act", lambda e: e.activation(out=dst2, in_=bank.ap, func=AF.Copy), [bank], [dst_buf])
                if RCUT < 2:
                    return
                a = ab_r.next()
                b = ab_r.next()
                a3 = a.ap.rearrange("p (h e) -> p h e", e=16)
                b3 = b.ap.rearrange("p (h e) -> p h e", e=16)
                OP("dve", lambda e: e.tensor_tensor(out=a3, in0=z3[:, :, 0:16], in1=cc, op=ALU.mult),
                   [bank, "rope"], [a])
                OP("dve", lambda e: e.tensor_tensor(out=b3, in0=z3[:, :, 0:16], in1=ss, op=ALU.mult),
                   [bank, "rope"], [b])
                if RCUT < 3:
                    return
                OP("pool", lambda e: e.tensor_tensor(out=dst3[:, :, 0:8], in0=a3[:, :, 0:8], in1=b3[:, :, 8:16],
                                                     op=ALU.subtract), [a, b], [dst_buf])
                OP("pool", lambda e: e.tensor_tensor(out=dst3[:, :, 8:16], in0=a3[:, :, 8:16], in1=b3[:, :, 0:8],
                                                     op=ALU.add), [a, b], [dst_buf])

            def gelu2(bank):
                sq = t512.next()
                OP("act", lambda e: e.activation(out=sq.ap, in_=bank.ap, func=AF.Square), [bank], [sq])
                w = t512.next()
                OP("dve", lambda e: e.scalar_tensor_tensor(out=w.ap, in0=sq.ap, scalar=1.0 / GC, in1=bank.ap,
                                                           op0=ALU.add, op1=ALU.mult), [sq, bank], [w])
                th = t512.next()
                OP("act", lambda e: e.activation(out=th.ap, in_=w.ap, func=AF.Tanh, scale=GK), [w], [th])
                o = t512.next()
                OP("dve", lambda e: e.scalar_tensor_tensor(out=o.ap, in0=th.ap, scalar=1.0, in1=bank.ap,
                                                           op0=ALU.add, op1=ALU.mult), [th, bank], [o])
                return o

            def silu2(bank):
                th = t512.next()
                OP("act", lambda e: e.activation(out=th.ap, in_=bank.ap, func=AF.Tanh, scale=0.5), [bank], [th])
                o = t512.next()
                OP("dve", lambda e: e.scalar_tensor_tensor(out=o.ap, in0=th.ap, scalar=1.0, in1=bank.ap,
                                                           op0=ALU.add, op1=ALU.mult), [th, bank], [o])
                return o

            def m1(t):
                kind = t["kind"]
                gt = t["gt"]
                full = kind != "halo"
                qkb = qkb_r.next()
                t["qkb"] = qkb
                if full:
                    bq = mm_group(t, 3)
                    rope_evac(t, bq, qkb.ap[:, 0:512], qkb)
                bk = mm_group(t, 4)
                if t.get("k_out") is not None:
                    kf = kf_r.next()
                    rope_evac(t, bk, kf.ap, kf)
                    OP("pool", lambda e: e.tensor_copy(out=qkb.ap[:, 512:1024], in_=kf.ap), [kf], [qkb])
                    DMA(lambda e: e.dma_start(out=t["k_out"], in_=kf.ap), reads=[kf])
                else:
                    rope_evac(t, bk, qkb.ap[:, 512:1024], qkb)
                bv = mm_group(t, 5)
                if t.get("v_out") is not None:
                    vf = t512.next()
                    OP("act", lambda e: e.activation(out=vf.ap, in_=bv.ap, func=AF.Copy), [bv], [vf])
                    DMA(lambda e: e.dma_start(out=t["v_out"], in_=vf.ap), reads=[vf])
                if kind != "sample":
                    vex = vex_r.next()
                    vex3 = vex.ap.rearrange("p (h e) -> p h e", e=65)
                    OP("dve", lambda e: e.tensor_copy(out=vex3[:, :, 0:64],
                                                      in_=bv.ap.rearrange("p (h e) -> p h e", e=64)), [bv], [vex])
                    OP("pool", lambda e: e.tensor_copy(out=vex3[:, :, 64:65],
                                                       in_=vld[:, gt:gt + 1].unsqueeze(1).broadcast_to([128, 8, 1])),
                       ["vld"], [vex])
                    DMA(lambda e: e.dma_start(out=v_scr[gt * 128:(gt + 1) * 128, :], in_=vex.ap), reads=[vex])
                if not full:
                    return
                bva = mm_group(t, 1)
                sq = t512.next()
                OP("act", lambda e: e.activation(out=sq.ap, in_=bva.ap, func=AF.Square), [bva], [sq])
                w = t512.next()
                OP("dve", lambda e: e.scalar_tensor_tensor(out=w.ap, in0=sq.ap, scalar=1.0 / GC, in1=bva.ap,
                                                           op0=ALU.add, op1=ALU.mult), [sq, bva], [w])
                th = t512.next()
                OP("act", lambda e: e.activation(out=th.ap, in_=w.ap, func=AF.Tanh, scale=GK), [w], [th])
                v2 = v2_r.next()
                OP("dve", lambda e: e.scalar_tensor_tensor(out=v2.ap, in0=th.ap, scalar=1.0, in1=bva.ap,
                                                           op0=ALU.add, op1=ALU.mult), [th, bva], [v2])
                st = st_r.next()
                OP("dve", lambda e: e.bn_stats(out=st.ap[:, 0:6], in_=v2.ap), [v2], [st])
                OP("dve", lambda e: e.bn_aggr(out=st.ap[:, 6:8], in_=st.ap[:, 0:6]), [st], [st])
                rh = rstd_chain(Buf(st.ap[:, 7:8], st.key), 0.25, EPS, 0.5)
                t.update(v2=v2, st=st, rh=rh)
                bu = mm_group(t, 0)
                u2 = gelu2(bu)
                bga = mm_group(t, 2)
                ga2 = silu2(bga)
                ug = ug_r.next()
                OP("pool", lambda e: e.tensor_tensor(out=ug.ap, in0=u2.ap, in1=ga2.ap, op=ALU.mult), [u2, ga2], [ug])
                t["ug"] = ug
                bgb = mm_group(t, 6)
                gb2 = silu2(bgb)
                ot = t["ot"]
                DMA(lambda e: e.dma_start(out=sgb_scr[ot * 128:(ot + 1) * 128, :], in_=gb2.ap), reads=[gb2])

            def m2a(t):
                if t["kind"] != "halo":
                    v2, st, rh = t["v2"], t["st"], t["rh"]
                    vn0 = t512.next()
                    OP("dve", lambda e: e.tensor_scalar(out=vn0.ap, in0=v2.ap, scalar1=st.ap[:, 6:7], scalar2=rh.ap,
                                                        op0=ALU.subtract, op1=ALU.mult), [v2, st, rh], [vn0])
                    vn1 = t512.next()
                    OP("pool", lambda e: e.tensor_tensor(out=vn1.ap, in0=vn0.ap, in1=lng[:], op=ALU.mult),
                       [vn0, "lng"], [vn1])
                    van = t512.next()
                    OP("pool", lambda e: e.tensor_tensor(out=van.ap, in0=vn1.ap, in1=lnb[:], op=ALU.add),
                       [vn1, "lnb"], [van])
                    vab = vab_r.next()
                    OP("act", lambda e: e.activation(out=vab.ap, in_=van.ap, func=AF.Copy), [van], [vab])
                    if t.get("va_out") is not None:
                        DMA(lambda e: e.dma_start(out=t["va_out"], in_=van.ap), reads=[van])
                    t["vab"] = vab

            def m2b(t):
                kind = t["kind"]
                full = kind != "halo"
                qkb = t["qkb"]
                vab = t.get("vab")
                lo = 0 if full else 4
                for j in range(lo, 8):
                    OP("pe", lambda e, j=j: e.transpose(out=p_qk.ap[:, j * 128:(j + 1) * 128],
                                                         in_=qkb.ap[:, j * 128:(j + 1) * 128],
                                                         identity=ident_bf[:]), [qkb, "ident_bf"], [p_qk])
                pe_fence([p_qk])
                if full:
                    qc = t["qt_col"]
                    OP("dve", lambda e: e.tensor_copy(out=QT[:, :, qc:qc + 128],
                                                      in_=p_qk.ap[:, 0:512].rearrange("p (j t) -> p j t", t=128)),
                       [p_qk], [("QT", qc)])
                kc0 = t["kt_col"]
                OP("act", lambda e: e.activation(out=KT[:, :, kc0:kc0 + 128],
                                                 in_=p_qk.ap[:, 512:1024].rearrange("p (j t) -> p j t", t=128),
                                                 func=AF.Copy), [p_qk], [("KT", kc0)])
                if not full:
                    return
                ug = t["ug"]
                wsp = wsps_bf if kind == "sample" else wsp_bf
                wkey = "wsps_bf" if kind == "sample" else "wsp_bf"
                for g in range(4):
                    OP("pe", lambda e, g=g: e.matmul(ps_sp.ap[:, g * 128:(g + 1) * 128], lhsT=wsp[:, g, :],
                                                      rhs=vab.ap[:, g * 128:(g + 1) * 128], start=True, stop=True),
                       [vab, wkey], [ps_sp])
                pe_fence([ps_sp])
                ta = t512.next()
                bs = bsps if kind == "sample" else bsp
                bskey = "bsps" if kind == "sample" else "bsp"
                for g in range(4):
                    OP("dve", lambda e, g=g: e.scalar_tensor_tensor(
                        out=ta.ap[:, g * 128:(g + 1) * 128], in0=ps_sp.ap[:, g * 128:(g + 1) * 128],
                        scalar=bs[:, g:g + 1], in1=ug.ap[:, g * 128:(g + 1) * 128], op0=ALU.add, op1=ALU.mult),
                       [ps_sp, ug, bskey], [ta])
                ot = t["ot"]
                DMA(lambda e: e.dma_start(out=ta_scr[ot * 128:(ot + 1) * 128, :], in_=ta.ap), reads=[ta])

            n = len(tiles)
            for i in range(n + 3):
                if i < n:
                    load(tiles[i])
                if 0 <= i - 1 < n:
                    norm(tiles[i - 1])
                if 0 <= i - 3 < n:
                    m2a(tiles[i - 3])
                if 0 <= i - 2 < n:
                    m1(tiles[i - 2])
                if 0 <= i - 1 < n:
                    normB(tiles[i - 1])
                if 0 <= i - 3 < n:
                    m2b(tiles[i - 3])
            S.barrier()

        SLOT = [0, 2, 1, 3]

        def phase_B(ui):
            A32.reset()
            A16.reset()
            pt_r = Ring([A16.take(1024, "pt%d" % i) for i in range(3)])
            vb_r = Ring([A16.take(520, "vb%d" % i) for i in range(8)])
            osb_r = Ring([A32.take(520, "osb%d" % i) for i in range(3)])
            s_bufs = [(psA[:], [PS["psA0"], PS["psA1"]]), (psB[:], [PS["psB0"], PS["psB1"]])]
            pO_ap = psO[:]
            pO_k = [PS["psO0"], PS["psO1"]]
            fence.update(ap=pb1[0:2, 1020:1024].bitcast(F32), wide=pb1[0:2, 0:512].bitcast(F32), key="pb1")

            work = []
            for c, d in ((0, 1), (1, 4), (2, 16)):
                nqb = 2048 // d // 128
                for r in range(d):
                    for qb in range(nqb):
                        work.append((c, d, r, qb))

            loaded = {}

            def kt_cols(d, r, kb):
                local = r + d * 128 * kb
                u = ui
                if local < 0:
                    local += 2048
                    u = ui - 1
                base = (u % 2) * 2048 + local
                return base, u * 2048 + local

            def ensure(c, d, r, kb):
                key = (c, r, kb)
                if key in loaded:
                    return loaded[key]
                vb = vb_r.next()
                for k2 in [k for k, v in loaded.items() if v is vb]:
                    del loaded[k2]
                _, g0 = kt_cols(d, r, kb)
                DMA(lambda e: e.dma_start(out=vb.ap, in_=v_scr[g0:g0 + d * 127 + 1:d, :]), writes=[vb])
                loaded[key] = vb
                return vb

            items = []
            for w, (c, d, r, qb) in enumerate(work):
                for g in range(2):
                    items.append((w, g))
            st_ = {}

            def stage_S(idx):
                w, g = items[idx]
                c, d, r, qb = work[w]
                if g == 0:
                    ensure(c, d, r, qb - 1)
                    ensure(c, d, r, qb)
                    for w2 in (w + 1, w + 2):
                        if w2 < len(work):
                            c2, d2, r2, qb2 = work[w2]
                            ensure(c2, d2, r2, qb2 - 1)
                            ensure(c2, d2, r2, qb2)
                sb_ap, sb_keys = s_bufs[idx % 2]
                s3 = sb_ap.rearrange("p (s n) -> p s n", n=256)
                q0 = r + d * 128 * qb
                kp, _ = kt_cols(d, r, qb - 1)
                kc, _ = kt_cols(d, r, qb)
                for hh in range(4):
                    h = 4 * g + hh
                    pr = h // 2
                    r0 = (h % 2) * 64
                    sl = SLOT[hh]
                    qap = QT[r0:r0 + 64, pr, q0:q0 + d * 127 + 1:d]
                    OP("pe", lambda e, sl=sl, r0=r0, pr=pr, qap=qap: e.matmul(
                        s3[:, sl, 0:128], lhsT=KT[r0:r0 + 64, pr, kp:kp + d * 127 + 1:d], rhs=qap,
                        start=True, stop=True), ["QTall", "KTall"], [sb_keys[sl // 2]])
                    OP("pe", lambda e, sl=sl, r0=r0, pr=pr, qap=qap: e.matmul(
                        s3[:, sl, 128:256], lhsT=KT[r0:r0 + 64, pr, kc:kc + d * 127 + 1:d], rhs=qap,
                        start=True, stop=True), ["QTall", "KTall"], [sb_keys[sl // 2]])
                pe_fence(sb_keys, wide=False)
                pt = pt_r.next()
                OP("act", lambda e: e.activation(out=pt.ap, in_=sb_ap, func=AF.Exp, scale=0.125), sb_keys, [pt])
                pt3 = pt.ap.rearrange("p (s n) -> p s n", n=256)
                OP("dve", lambda e: e.tensor_tensor(out=pt3, in0=pt3,
                                                    in1=bmask_bf[:].unsqueeze(1).broadcast_to([128, 4, 256]),
                                                    op=ALU.mult), [pt, "bmask_bf"], [pt])
                st_[idx] = pt

            def stage_P(idx):
                w, g = items[idx]
                c, d, r, qb = work[w]
                pt = st_.pop(idx)
                pt3 = pt.ap.rearrange("p (s n) -> p s n", n=256)
                vp = loaded[(c, r, qb - 1)]
                vc = loaded[(c, r, qb)]
                vp3 = vp.ap.rearrange("p (h e) -> p h e", e=65)
                vc3 = vc.ap.rearrange("p (h e) -> p h e", e=65)
                for hh in range(4):
                    h = 4 * g + hh
                    sl = SLOT[hh]
                    oc = g * 512 + hh * 65
                    OP("pe", lambda e, sl=sl, h=h, oc=oc, hh=hh: e.matmul(
                        pO_ap[:, oc:oc + 65], lhsT=pt3[:, sl, 0:128], rhs=vp3[:, h, :],
                        start=(hh == 0), stop=False, skip_group_check=True), [pt, vp], [pO_k[g]])
                    OP("pe", lambda e, sl=sl, h=h, oc=oc: e.matmul(
                        pO_ap[:, oc:oc + 65], lhsT=pt3[:, sl, 128:256], rhs=vc3[:, h, :],
                        start=False, stop=True, skip_group_check=True), [pt, vc], [pO_k[g]])
                pe_fence([pO_k[g]])
                if g == 1:
                    osb = osb_r.next()
                    OP("act", lambda e: e.activation(out=osb.ap[:, 0:260], in_=pO_ap[:, 0:260], func=AF.Copy),
                       [pO_k[0]], [(osb.key, 0)])
                    OP("dve", lambda e: e.tensor_copy(out=osb.ap[:, 260:520], in_=pO_ap[:, 512:772]),
                       [pO_k[1]], [(osb.key, 1)])
                    q0 = (ui - 1) * 2048 + r + d * 128 * qb
                    DMA(lambda e: e.dma_start(out=o_scr[c, q0:q0 + d * 127 + 1:d, :], in_=osb.ap),
                        reads=[(osb.key, 0), (osb.key, 1)])

            n = len(items)
            for idx in range(n + 1):
                if idx < n:
                    stage_S(idx)
                if idx - 1 >= 0:
                    stage_P(idx - 1)
            S.barrier()

        def phase_C(tiles):
            A32.reset()
            A16.reset()
            xt_r = Ring([A32.take(1024, "cx%d" % i) for i in range(2)])
            ta_r = Ring([A32.take(512, "cta%d" % i) for i in range(2)])
            sg_r = Ring([A32.take(512, "csg%d" % i) for i in range(2)])
            o_r = Ring([A32.take(1560, "co%d" % i) for i in range(2)])
            tb_r = Ring([A32.take(512, "ctb%d" % i) for i in range(1)])
            y_r = Ring([A32.take(1024, "cy%d" % i) for i in range(2)])
            rd_r = Ring([A32.take(8, "crd%d" % i) for i in range(2)])
            hc_r = Ring([A16.take(1024, "chc%d" % i) for i in range(2)])
            hT_r = Ring([A16.take(1024, "chT%d" % i) for i in range(2)])
            jk = A16.take(1024, "cjunk")
            jk2 = A16.take(1024, "cjunk2")
            p_hT = PS["pb0"]
            ybanks = Ring([(psA[:], [PS["psA0"], PS["psA1"]]), (psB[:], [PS["psB0"], PS["psB1"]])])
            fence.update(ap=pb1[0:2, 1020:1024].bitcast(F32), wide=pb1[0:2, 0:512].bitcast(F32), key="pb1")

            def load(t):
                ta = ta_r.next()
                sg = sg_r.next()
                ob = o_r.next()
                ot = t["ot"]
                DMA(lambda e: e.dma_start(out=ta.ap, in_=ta_scr[ot * 128:(ot + 1) * 128, :]), writes=[ta])
                DMA(lambda e: e.dma_start(out=sg.ap, in_=sgb_scr[ot * 128:(ot + 1) * 128, :]), writes=[sg])
                if t["kind"] == "sample":
                    DMA(lambda e: e.dma_start(out=ob.ap[:, 0:520], in_=os_scr), writes=[ob])
                else:
                    for c in range(3):
                        DMA(lambda e, c=c: e.dma_start(out=ob.ap[:, c * 520:(c + 1) * 520],
                                                       in_=o_scr[c, ot * 128:(ot + 1) * 128, :]), writes=[ob])
                t.update(ta=ta, sg=sg, ob=ob)

            def load_x(t):
                xt = xt_r.next()
                DMA(lambda e: e.dma_start(out=xt.ap, in_=t["src"]), writes=[xt])
                t["xt"] = xt

            def stage1(t):
                ta, sg, ob = t["ta"], t["sg"], t["ob"]
                if t["kind"] != "sample":
                    OP("dve", lambda e: e.tensor_tensor(out=ob.ap[:, 0:520], in0=ob.ap[:, 0:520],
                                                        in1=ob.ap[:, 520:1040], op=ALU.add), [ob], [ob])
                    OP("dve", lambda e: e.tensor_tensor(out=ob.ap[:, 0:520], in0=ob.ap[:, 0:520],
                                                        in1=ob.ap[:, 1040:1560], op=ALU.add), [ob], [ob])
                o3 = ob.ap[:, 0:520].rearrange("p (h e) -> p h e", e=65)
                rd = rd_r.next()
                OP("dve", lambda e: e.reciprocal(out=rd.ap.unsqueeze(2), in_=o3[:, :, 64:65]), [ob], [rd])
                tb = tb_r.next()
                tb3 = tb.ap.rearrange("p (h e) -> p h e", e=64)
                OP("dve", lambda e: e.tensor_tensor(out=tb3, in0=o3[:, :, 0:64],
                                                    in1=rd.ap.unsqueeze(2).broadcast_to([128, 8, 64]), op=ALU.mult),
                   [ob, rd], [tb])
                OP("pool", lambda e: e.tensor_tensor(out=tb.ap, in0=tb.ap, in1=sg.ap, op=ALU.mult), [tb, sg], [tb])
                ssa = new_scalar()
                ssb = new_scalar()
                OP("act", lambda e: e.activation(out=jk.ap[:, 0:512], in_=ta.ap, func=AF.Square, accum_out=ssa.ap),
                   [ta], [jk, ssa])
                OP("act", lambda e: e.activation(out=jk.ap[:, 512:1024], in_=tb.ap, func=AF.Square, accum_out=ssb.ap),
                   [tb], [jk, ssb])
                ra = rstd_chain(ssa, 1.0 / (16.0 * 512.0), EPS, 0.25)
                rb = rstd_chain(ssb, 1.0 / (4.0 * 512.0), EPS, 0.5)
                hc = hc_r.next()
                OP("act", lambda e: e.activation(out=hc.ap[:, 0:512], in_=ta.ap, func=AF.Copy, scale=ra.ap),
                   [ta, ra], [hc])
                OP("act", lambda e: e.activation(out=hc.ap[:, 512:1024], in_=tb.ap, func=AF.Copy, scale=rb.ap),
                   [tb, rb], [hc])
                t["hc"] = hc

            def stage2a(t):
                hc = t["hc"]
                for k in range(8):
                    OP("pe", lambda e, k=k: e.transpose(out=p_hT.ap[:, k * 128:(k + 1) * 128],
                                                         in_=hc.ap[:, k * 128:(k + 1) * 128], identity=ident_bf[:]),
                       [hc, "ident_bf"], [p_hT])
                pe_fence([p_hT])
                hT = hT_r.next()
                OP("dve", lambda e: e.tensor_copy(out=hT.ap, in_=p_hT.ap), [p_hT], [hT])
                yb_ap, yb_k = ybanks.next()
                for cg in range(2):
                    for kc in range(8):
                        OP("pe", lambda e, cg=cg, kc=kc: e.matmul(
                            yb_ap[:, cg * 512:(cg + 1) * 512], lhsT=hT.ap[:, kc * 128:(kc + 1) * 128],
                            rhs=w_out_bf[:, kc, cg * 512:(cg + 1) * 512], start=(kc == 0), stop=(kc == 7)),
                           [hT, ("w_out_bf", kc, 0), ("w_out_bf", kc, 1)], [yb_k[cg]])
                pe_fence(yb_k)
                t.update(yb_ap=yb_ap, yb_k=yb_k)

            def stage2b(t):
                xt, yb_ap, yb_k = t["xt"], t["yb_ap"], t["yb_k"]
                y = y_r.next()
                for cg in range(2):
                    OP("dve", lambda e, cg=cg: e.tensor_tensor(out=y.ap[:, cg * 512:(cg + 1) * 512],
                                                               in0=yb_ap[:, cg * 512:(cg + 1) * 512],
                                                               in1=xt.ap[:, cg * 512:(cg + 1) * 512], op=ALU.add),
                       [yb_k[cg], xt], [(y.key, cg)])
                ssy = new_scalar()
                yk = [(y.key, 0), (y.key, 1)]
                OP("act", lambda e: e.activation(out=jk2.ap, in_=y.ap, func=AF.Square, accum_out=ssy.ap),
                   yk, [jk2, ssy])
                t["ry"] = rstd_chain(ssy, 1.0 / D, EPS, 1.0)
                t["y"] = y

            def stage3(t):
                y, ry = t["y"], t["ry"]
                yk = [(y.key, 0), (y.key, 1)]
                OP("act", lambda e: e.activation(out=y.ap, in_=y.ap, func=AF.Copy, scale=ry.ap), yk + [ry], yk)
                OP("pool", lambda e: e.tensor_tensor(out=y.ap, in0=y.ap, in1=fing[:], op=ALU.mult),
                   yk + ["fing"], yk)
                DMA(lambda e: e.dma_start(out=t["dst"], in_=y.ap), reads=yk)

            n = len(tiles)
            for st in range(n + 3):
                if st < n:
                    load(tiles[st])
                if 0 <= st - 1 < n:
                    load_x(tiles[st - 1])
                if 0 <= st - 2 < n:
                    stage2a(tiles[st - 2])
                if 0 <= st - 1 < n:
                    stage1(tiles[st - 1])
                if 0 <= st - 2 < n:
                    stage2b(tiles[st - 2])
                if 0 <= st - 3 < n:
                    stage3(tiles[st - 3])
            S.barrier()

        def phase_SA():
            A32.reset()
            A16.reset()
            kc_r = Ring([A32.take(512, "skc%d" % i) for i in range(8)])
            vc_r = Ring([A32.take(512, "svc%d" % i) for i in range(8)])
            os_r = Ring([A32.take(520, "sos%d" % i) for i in range(2)])
            pf_r = Ring([A32.take(64, "spf%d" % i) for i in range(3)])
            kts_r = Ring([A16.take(512, "skt%d" % i) for i in range(3)])
            vbs = [A16.take(520, "svb%d" % i) for i in range(4)]
            vb_r = Ring(vbs)
            pm_r = Ring([A16.take(64, "spm%d" % i) for i in range(3)])
            qbd = A16.take(4 * 16 * 16, "qbd")
            qbd4 = qbd.ap.rearrange("p (j b c) -> p j b c", j=4, b=16)
            ktp = Ring([PS["psA0"], PS["psA1"]])
            sps = Ring([Buf(psB[:, 0:64], "psB0"), Buf(psB[:, 512:576], "psB1")])
            pO_ap = psO[:]
            pO_k = [PS["psO0"], PS["psO1"]]
            fence.update(ap=pb1[0:2, 1020:1024].bitcast(F32), wide=pb1[0:2, 0:512].bitcast(F32), key="pb1")

            OP("pool", lambda e: e.memset(qbd.ap, 0.0), writes=[qbd])
            q4 = QT[:, :, 0:128].rearrange("p j (b t) -> p j b t", t=8)
            OP("dve", lambda e: e.tensor_copy(out=qbd4[0:64, :, :, 0:8], in_=q4[0:64]), ["QTall", qbd], [qbd])
            OP("dve", lambda e: e.tensor_copy(out=qbd4[64:128, :, :, 8:16], in_=q4[64:128]), ["QTall", qbd], [qbd])
            for vb in vbs:
                OP("pool", lambda e, vb=vb: e.memset(vb.ap, 1.0), writes=[vb])

            items = []
            for b in range(16):
                for t0 in range(8):
                    items.append((b, len(items) % 13, 96, ck[b, t0:1536:16, :], cv[b, t0:1536:16, :]))
                for c in range(4):
                    items.append((b, 8 + c, 128, ck[b, 1536 + 128 * c:1664 + 128 * c, :],
                                  cv[b, 1536 + 128 * c:1664 + 128 * c, :]))
                items.append((b, 12, 8, kn_o[b * 8:(b + 1) * 8, :], vn_o[b * 8:(b + 1) * 8, :]))
            n_it = len(items)
            stt = [dict() for _ in range(n_it)]

            def st_load(j):
                b, kt, nk, ksrc, vsrc = items[j]
                kc = kc_r.next()
                vc = vc_r.next()
                DMA(lambda e: e.dma_start(out=kc.ap[0:nk, :], in_=ksrc), writes=[kc])
                DMA(lambda e: e.dma_start(out=vc.ap[0:nk, :], in_=vsrc), writes=[vc])
                stt[j].update(kc=kc, vc=vc)

            def st_T(j):
                b, kt, nk, ksrc, vsrc = items[j]
                kc, vc = stt[j]["kc"], stt[j]["vc"]
                kp = ktp.next()
                for q in range(4):
                    OP("pe", lambda e, q=q: e.transpose(
                        out=kp.ap[:, q * 128:q * 128 + nk], in_=kc.ap[0:nk, q * 128:(q + 1) * 128],
                        identity=ident_f[0:nk, 0:nk]), [kc, "ident_f"], [kp])
                pe_fence([kp])
                kts = kts_r.next()
                kts3 = kts.ap.rearrange("p (j n) -> p j n", n=128)
                OP("act", lambda e: e.activation(
                    out=kts3[:, :, 0:nk], in_=kp.ap.rearrange("p (j n) -> p j n", n=128)[:, :, 0:nk],
                    func=AF.Copy), [kp], [kts])
                vb = vb_r.next()
                OP("dve", lambda e: e.tensor_copy(
                    out=vb.ap[0:nk, :].rearrange("p (h e) -> p h e", e=65)[:, :, 0:64],
                    in_=vc.ap[0:nk, :].rearrange("p (h e) -> p h e", e=64)), [vc], [vb])
                stt[j].update(kts=kts, kts3=kts3, vb=vb)

            def st_S(j):
                b, kt, nk, ksrc, vsrc = items[j]
                kts, kts3 = stt[j]["kts"], stt[j]["kts3"]
                sp_ = sps.next()
                for q in range(4):
                    OP("pe", lambda e, q=q: e.matmul(
                        sp_.ap[0:nk, q * 16:(q + 1) * 16], lhsT=kts3[:, q, 0:nk], rhs=qbd4[:, q, b, :],
                        start=True, stop=True), [kts, qbd], [sp_])
                pe_fence([sp_])
                pf = pf_r.next()
                OP("act", lambda e: e.activation(out=pf.ap[0:nk, :], in_=sp_.ap[0:nk, :], func=AF.Exp, scale=0.125),
                   [sp_], [pf])
                pm = pm_r.next()
                OP("dve", lambda e: e.tensor_tensor(
                    out=pm.ap[0:nk, :].rearrange("p (h t) -> p h t", t=8),
                    in0=pf.ap[0:nk, :].rearrange("p (h t) -> p h t", t=8),
                    in1=smask[0:nk, kt, :].unsqueeze(1).broadcast_to([nk, 8, 8]), op=ALU.mult),
                   [pf, "smask"], [pm])
                stt[j].update(pm=pm)

            def st_PV(j):
                b, kt, nk, ksrc, vsrc = items[j]
                pm, vb = stt[j]["pm"], stt[j]["vb"]
                vb3 = vb.ap.rearrange("p (h e) -> p h e", e=65)
                for h in range(8):
                    oc = (h // 4) * 512 + (h % 4) * 65
                    OP("pe", lambda e, h=h, oc=oc: e.matmul(
                        pO_ap[0:8, oc:oc + 65], lhsT=pm.ap[0:nk, h * 8:(h + 1) * 8], rhs=vb3[0:nk, h, :],
                        start=(kt == 0 and h % 4 == 0), stop=(kt == 12), skip_group_check=True),
                       [pm, vb], [pO_k[h // 4]])
                if kt == 12:
                    pe_fence(pO_k)
                    osb = os_r.next()
                    OP("act", lambda e: e.activation(out=osb.ap[0:8, 0:260], in_=pO_ap[0:8, 0:260], func=AF.Copy),
                       [pO_k[0]], [(osb.key, 0)])
                    OP("dve", lambda e: e.tensor_copy(out=osb.ap[0:8, 260:520], in_=pO_ap[0:8, 512:772]),
                       [pO_k[1]], [(osb.key, 1)])
                    DMA(lambda e: e.dma_start(out=os_scr[b * 8:(b + 1) * 8, :], in_=osb.ap[0:8, :]),
                        reads=[(osb.key, 0), (osb.key, 1)])
                stt[j] = None

            LA = 6
            for j in range(min(LA, n_it)):
                st_load(j)
            for st in range(n_it + 2):
                if st + LA < n_it:
                    st_load(st + LA)
                if st < n_it:
                    st_T(st)
                if 0 <= st - 1 < n_it:
                    st_S(st - 1)
                if 0 <= st - 2 < n_it:
                    st_PV(st - 2)
            S.barrier()

        def own_tile(ot):
            gt = 16 + ot
            ui = 1 + ot // 16
            t = dict(kind="own", gt=gt, ot=ot, src=xp[gt * 128:(gt + 1) * 128, :],
                     kt_col=(ui % 2) * 2048 + (ot % 16) * 128, qt_col=(ot % 16) * 128,
                     dst=y_o[ot * 128:(ot + 1) * 128, :])
            if ot >= 16:
                t["k_out"] = kwin_o[(ot - 16) * 128:(ot - 15) * 128, :]
                t["v_out"] = vwin_o[(ot - 16) * 128:(ot - 15) * 128, :]
            return t

        halo = [dict(kind="halo", gt=i, src=xp[i * 128:(i + 1) * 128, :], kt_col=i * 128) for i in range(16)]
        samp = dict(kind="sample", gt=48, ot=32, src=xs, kt_col=0, qt_col=0, k_out=kn_o, v_out=vn_o,
                    va_out=vas_o, dst=ys_o)

        if dbg is not None:
            if dbg.get("halo", 0):
                phase_A(halo[:dbg["halo"]])
            if dbg.get("own"):
                phase_A([own_tile(ot) for ot in dbg["own"]])
            if dbg.get("samp"):
                phase_A([samp])
            if dbg.get("C"):
                phase_C([own_tile(ot) for ot in dbg["C"]])
            S.barrier()
            sems = {n: es.enter_context(nc.semaphore(n)) for n in sorted(S.sem_names)}
            with nc.Block() as block:
                S.emit(block, sems)
            return nc
        if "A" in phases:
            phase_A(halo)
        for u in range(int(os.environ.get("K_NUNITS", "2"))):
            tl = [own_tile(ot) for ot in range(16 * u, 16 * u + 16)]
            if "A" in phases:
                phase_A(tl)
            if "B" in phases:
                phase_B(u + 1)
            if "C" in phases:
                phase_C(tl)
        if "S" in phases:
            phase_A([samp])
            phase_SA()
            phase_C([samp])
        S.barrier()

        sems = {n: es.enter_context(nc.semaphore(n)) for n in sorted(S.sem_names)}
        with nc.Block() as block:
            S.emit(block, sems)
    return nc


def _rope_table(core):
    half = core % 2
    pos = np.zeros((NT_ALL, 128), np.float32)
    p = np.arange(128, dtype=np.float32)
    for ti in range(16):
        pos[ti] = (2048 + ti * 128 + p) if half == 1 else 0.0
    for ot in range(32):
        pos[16 + ot] = half * 4096 + ot * 128 + p
    pos[48] = 8192 + (np.arange(128) % 8)
    inv = (np.float32(500000.0) ** (-np.arange(0, 16, 2, dtype=np.float32) / np.float32(16))).astype(np.float32)
    ang = (pos[:, :, None] * inv[None, None, :]).astype(np.float32)
    cos = np.cos(ang).astype(np.float32)
    sin = np.sin(ang).astype(np.float32)
    tab = np.concatenate([cos, cos, sin, sin], axis=-1)
    return np.ascontiguousarray(tab.transpose(1, 0, 2))


def _sample_mask():
    m = np.zeros((128, 13, 8), np.float32)
    t = np.arange(8)
    for t0 in range(8):
        m[:96, t0, t0] = 1.0
    for c in range(4):
        i = np.arange(128)
        dist = 512 + t[None, :] - 128 * c - i[:, None]
        cnt = ((dist >= 0) & (dist <= 128)).astype(np.float32)
        cnt += ((dist >= 0) & (dist % 4 == 0) & (dist <= 512)).astype(np.float32)
        cnt += ((dist >= 0) & (dist % 16 == 0) & (dist <= 2048)).astype(np.float32)
        m[:, 8 + c, :] = cnt
    tp = np.arange(8)
    dist = t[None, :] - tp[:, None]
    cnt = (dist >= 0).astype(np.float32) + ((dist >= 0) & (dist % 4 == 0)) + ((dist >= 0) & (dist % 16 == 0))
    m[:8, 12, :] = cnt
    return m


_NC_CACHE = {}


def kernel(x_prompt, x_sample, cache_k_win, cache_v_win, norm_g, w_in, ln_v_g, ln_v_b,
           w_spatial, b_spatial, gn_a, gn_b, w_out, final_g, _phases="ABCS", _same_engine_sync=True):
    f32 = np.float32
    x_prompt = np.asarray(x_prompt, f32)
    x_sample = np.asarray(x_sample, f32)
    cache_k_win = np.asarray(cache_k_win, f32)
    cache_v_win = np.asarray(cache_v_win, f32)
    w_in = np.ascontiguousarray(np.asarray(w_in, f32))
    w_out = np.ascontiguousarray(np.asarray(w_out, f32))
    w_spatial = np.asarray(w_spatial, f32)
    b_spatial = np.asarray(b_spatial, f32)

    key = (_phases, _same_engine_sync)
    if key not in _NC_CACHE:
        _NC_CACHE[key] = build_nc(_phases, _same_engine_sync)
    nc = _NC_CACHE[key]

    g_in = np.ascontiguousarray(np.asarray(norm_g, f32).reshape(8, 128).T)
    g_out = np.ascontiguousarray(np.concatenate([np.asarray(gn_a, f32), np.asarray(gn_b, f32)]).reshape(8, 128).T)
    lng = np.ascontiguousarray(np.broadcast_to(np.asarray(ln_v_g, f32)[None, :], (128, 512)))
    lnb = np.ascontiguousarray(np.broadcast_to(np.asarray(ln_v_b, f32)[None, :], (128, 512)))
    fing = np.ascontiguousarray(np.broadcast_to(np.asarray(final_g, f32)[None, :], (128, 1024)))
    wspT = np.ascontiguousarray(w_spatial.transpose(2, 0, 1))
    wsps = np.zeros((128, 4, 128), f32)
    w8 = w_spatial[:, :8, :8].transpose(2, 0, 1)
    for b in range(16):
        wsps[b * 8:(b + 1) * 8, :, b * 8:(b + 1) * 8] = w8
    jj = np.arange(128)
    mtril = (jj[:, None] <= jj[None, :]).astype(f32)
    mtrils = ((jj[:, None] // 8 == jj[None, :] // 8) & (jj[:, None] % 8 <= jj[None, :] % 8)).astype(f32)
    bsp = np.ascontiguousarray(b_spatial.T)
    bsps = np.ascontiguousarray(np.tile(b_spatial[:, :8].T, (16, 1)))
    bmask = np.concatenate([(jj[:, None] >= jj[None, :]), (jj[:, None] <= jj[None, :])], axis=1).astype(f32)
    smask = _sample_mask()
    ident = np.eye(128, dtype=f32)

    in_maps = []
    for c in range(NCORES):
        b, half = c // 2, c % 2
        xp = np.empty((6144, D), f32)
        if half == 1:
            xp[:2048] = x_prompt[b, 2048:4096]
            vld = np.ones((128, 48), f32)
        else:
            xp[:2048] = 0.0
            vld = np.ones((128, 48), f32)
            vld[:, :16] = 0.0
        xp[2048:] = x_prompt[b, half * 4096:(half + 1) * 4096]
        in_maps.append(dict(
            xp=xp, xs=np.ascontiguousarray(x_sample[16 * c:16 * c + 16].reshape(128, D)),
            ck=np.ascontiguousarray(cache_k_win[16 * c:16 * c + 16].reshape(16, 2048, 512)),
            cv=np.ascontiguousarray(cache_v_win[16 * c:16 * c + 16].reshape(16, 2048, 512)),
            w_in=w_in, w_out=w_out, g_in=g_in, g_out=g_out, lng=lng, lnb=lnb, fing=fing,
            wspT=wspT, wsps=wsps, mtril=mtril, mtrils=mtrils, bsp=bsp, bsps=bsps,
            rope=_rope_table(c), vld=vld, bmask=bmask, smask=smask, ident=ident))

    res = run_bass_kernel_spmd(nc, in_maps, core_ids=list(range(NCORES)))
    R = res.results

    y_prompt = np.empty((4, 8192, D), f32)
    k_win = np.empty((4, 2048, 8, 64), f32)
    v_win = np.empty((4, 2048, 8, 64), f32)
    for c in range(NCORES):
        b, half = c // 2, c % 2
        y_prompt[b, half * 4096:(half + 1) * 4096] = R[c]["y"]
        if half == 1:
            k_win[b] = R[c]["kwin"].reshape(2048, 8, 64)
            v_win[b] = R[c]["vwin"].reshape(2048, 8, 64)
    y_sample = np.concatenate([R[c]["ys"].reshape(16, 8, D) for c in range(NCORES)], axis=0)
    k_new = np.concatenate([R[c]["kn"].reshape(16, 8, 8, 64) for c in range(NCORES)], axis=0)
    v_new = np.concatenate([R[c]["vn"].reshape(16, 8, 8, 64) for c in range(NCORES)], axis=0)
    vas = np.concatenate([R[c]["vas"].reshape(16, 8, 512) for c in range(NCORES)], axis=0)
    return (y_prompt, y_sample, k_win, v_win, k_new, v_new, vas)
```
